# Optimizing a Trainium2 kernel written in Bass

```python
import math
import jax
import jax.numpy as jnp
from jax import lax
import numpy as np

D_MODEL = 1024
BATCH = 16
SEQ = 4096
DEPTH = 4

CHUNK = 64
N_EVEN = (DEPTH + 1) // 2
N_ODD = DEPTH // 2
D_FF = 2816
NORM_EPS = 1e-6

RW_HEADS = 8
RW_HD = 64
RW_W = RW_HEADS * RW_HD
W_LORA = 32
A_LORA = 32
V_LORA = 32
G_LORA = 96
RW_COLS = 3 * RW_W + W_LORA + A_LORA + G_LORA
GN_EPS = 64e-5

SB_HEADS = 8
SB_HD = 64
SB_W = SB_HEADS * SB_HD
SB_BLOCK = 128

EVEN_IN = RW_COLS + 3 * SB_W
EVEN_MIX = RW_W + SB_W

S5_GROUP = 16
S5_G = 16
S5_W = S5_G * S5_GROUP
S5_P = 64
DT_MIN = 1e-3
DT_MAX = 1e-1

HG_HEADS = 6
HG_DK = 128
HG_DV = 128
HG_KW = HG_HEADS * HG_DK
HG_VW = HG_HEADS * HG_DV
HG_BLOCK = CHUNK // 4

ODD_IN = S5_W + 2 * HG_KW + 2 * HG_VW
ODD_MIX = S5_W + HG_VW

kernel_name = "hybrid_rwkv7_stickbreak_s5_hgrn2_macaron"


def rmsnorm(x, gain):
    xf = x.astype(jnp.float32)
    y = xf * lax.rsqrt(jnp.mean(xf * xf, axis=-1, keepdims=True) + NORM_EPS)
    return (y * gain.astype(jnp.float32)).astype(x.dtype)


def head_rmsnorm(t, gain):
    tf = t.astype(jnp.float32)
    return tf * lax.rsqrt(jnp.mean(tf * tf, axis=-1, keepdims=True) + NORM_EPS) * gain.astype(jnp.float32)


def swiglu_ffn(h, w13, w2):
    gate, up = jnp.split(h @ w13, 2, axis=-1)
    return (jax.nn.silu(gate) * up) @ w2


def token_shift(p):
    return jnp.pad(p, ((0, 0), (1, 0), (0, 0)))[:, :-1]


def rwkv7_recurrence(r, w, k, v, a_vec, b_vec):
    bsz, _, heads, n = r.shape

    def step(state, inp):
        r_t, w_t, k_t, v_t, a_t, b_t = inp
        sa = jnp.einsum('bhvk,bhk->bhv', state, a_t)
        state = (state * w_t[:, :, None, :] + sa[..., None] * b_t[:, :, None, :]
                 + v_t[..., None] * k_t[:, :, None, :])
        return state, jnp.einsum('bhvk,bhk->bhv', state, r_t)

    xs = tuple(jnp.moveaxis(t, 1, 0) for t in (r, w, k, v, a_vec, b_vec))
    s0 = jnp.zeros((bsz, heads, n, n), jnp.float32)
    _, ys = lax.scan(step, s0, xs)
    return jnp.moveaxis(ys, 0, 1)


def stick_breaking_attention(q, k, v):
    seq = q.shape[2]
    scale = q.shape[-1] ** -0.5
    outs = []
    for start in range(0, seq, SB_BLOCK):
        end = start + SB_BLOCK
        z = jnp.einsum('bhqd,bhkd->bhqk', q[:, :, start:end], k[:, :, :end]) * scale
        t_pos = start + jnp.arange(SB_BLOCK)[:, None]
        s_pos = jnp.arange(end)[None, :]
        before = s_pos < t_pos
        log_keep = jnp.where(before, jax.nn.log_sigmoid(-z), 0.0)
        log_stick = lax.cumsum(log_keep, axis=3, reverse=True) - log_keep
        weight = jnp.where(before, jnp.exp(jax.nn.log_sigmoid(z) + log_stick), 0.0)
        outs.append(jnp.einsum('bhqk,bhkd->bhqd', weight, v[:, :, :end]))
    return jnp.concatenate(outs, axis=2)


def _linear_combine(left, right):
    a_l, b_l = left
    a_r, b_r = right
    return a_r * a_l, a_r * b_l + b_r


def s5_layer(u, a_re, a_im, b_re, b_im, c_re, c_im, d, log_dt, w_glu, b_glu):
    bsz, seq, _ = u.shape
    f32 = jnp.float32
    ug = u.reshape(bsz, seq, S5_G, S5_GROUP)
    lam = lax.complex(jnp.minimum(a_re.astype(f32), -1e-4), a_im.astype(f32))
    dt = jnp.exp(log_dt.astype(f32))[:, None]
    lam_bar = jnp.exp(lam * dt)
    b_bar = ((lam_bar - 1.0) / lam)[..., None] * lax.complex(b_re.astype(f32), b_im.astype(f32))
    bu = jnp.einsum('gpc,bsgc->bsgp', b_bar, ug.astype(jnp.complex64))
    lam_seq = jnp.broadcast_to(lam_bar, (1, seq) + lam_bar.shape)
    _, states = lax.associative_scan(_linear_combine, (lam_seq, bu), axis=1)
    c = lax.complex(c_re.astype(f32), c_im.astype(f32))
    y = jnp.einsum('gcp,bsgp->bsgc', c, states).real.reshape(bsz, seq, S5_W) + d * u
    y = jax.nn.gelu(y)
    return y * jax.nn.sigmoid(y @ w_glu + b_glu)


def hgrn2_chunkwise(q, f_logit, i_val, lb):
    bsz, seq, _ = q.shape
    n_chunks = seq // HG_BLOCK
    log_f = jnp.logaddexp(jnp.log(lb), jnp.log1p(-lb) + jax.nn.log_sigmoid(f_logit))
    k = (1.0 - lb) * jax.nn.sigmoid(-f_logit)

    def chunks(t, dim):
        return t.reshape(bsz, n_chunks, HG_BLOCK, HG_HEADS, dim).transpose(1, 0, 3, 2, 4)

    qc, kc, gc, vc = chunks(q, HG_DK), chunks(k, HG_DK), chunks(log_f, HG_DK), chunks(i_val, HG_DV)
    g_cum = jnp.cumsum(gc, axis=3)
    g_last = g_cum[:, :, :, -1:, :]
    q_in = qc * jnp.exp(g_cum)
    k_in = kc * jnp.exp(-g_cum)
    k_end = kc * jnp.exp(g_last - g_cum)
    decay_end = jnp.exp(g_last[:, :, :, 0, :])
    causal = jnp.tril(jnp.ones((HG_BLOCK, HG_BLOCK), dtype=bool))

    def step(state, xs):
        q_c, k_c, ke_c, v_c, d_c = xs
        att = jnp.where(causal, jnp.einsum('bhtk,bhsk->bhts', q_c, k_c), 0.0)
        o = jnp.einsum('bhtk,bhkv->bhtv', q_c, state) + jnp.einsum('bhts,bhsv->bhtv', att, v_c)
        state = d_c[..., None] * state + jnp.einsum('bhsk,bhsv->bhkv', ke_c, v_c)
        return state, o

    s0 = jnp.zeros((bsz, HG_HEADS, HG_DK, HG_DV), jnp.float32)
    _, o = lax.scan(step, s0, (q_in, k_in, k_end, vc, decay_end))
    return o.transpose(1, 0, 3, 2, 4).reshape(bsz, seq, HG_HEADS, HG_DV)


def even_mixer(h, w_in, w_out, mu, w0, w2, a0, a2, g2, k_k, k_a, r_k, ln_w, ln_b,
               q_gain, k_gain, v_first, v_mix):
    bsz, seq, _ = h.shape
    proj = (h @ w_in).astype(jnp.float32)
    rw, sb = proj[..., :RW_COLS], proj[..., RW_COLS:]

    rw = rw + mu * (token_shift(rw) - rw)
    cuts = [RW_W, 2 * RW_W, 3 * RW_W, 3 * RW_W + W_LORA, 3 * RW_W + W_LORA + A_LORA]
    r, k, v, w_d, a_d, g_d = jnp.split(rw, cuts, axis=-1)
    log_w = -jax.nn.softplus(-(w0 + jnp.tanh(w_d) @ w2)) - 0.5
    decay = jnp.exp(-jnp.exp(log_w))
    a = jax.nn.sigmoid(a0 + a_d @ a2)
    g = jax.nn.sigmoid(g_d) @ g2
    if v_mix is not None:
        v0, v1, v2 = v_mix
        v = v + (v_first - v) * jax.nn.sigmoid(v0 + (h @ v1) @ v2)

    def heads(t):
        return t.reshape(bsz, seq, RW_HEADS, RW_HD)

    kk = heads(k * k_k)
    kk = kk * lax.rsqrt(jnp.maximum(jnp.sum(kk * kk, axis=-1, keepdims=True), 1e-24))
    k = k * (1.0 + (a - 1.0) * k_a)
    rh, kh, vh = heads(r), heads(k), heads(v)
    y = rwkv7_recurrence(rh, heads(decay), kh, vh, -kk, kk * heads(a))
    mean = jnp.mean(y, axis=-1, keepdims=True)
    var = jnp.mean(jnp.square(y - mean), axis=-1, keepdims=True)
    y = ((y - mean) * lax.rsqrt(var + GN_EPS)).reshape(bsz, seq, RW_W) * ln_w + ln_b
    bonus = jnp.sum(rh * kh * r_k.reshape(RW_HEADS, RW_HD), axis=-1, keepdims=True) * vh
    y_rw = (y + bonus.reshape(bsz, seq, RW_W)) * g

    def to_bhsd(t):
        return t.reshape(bsz, seq, SB_HEADS, SB_HD).transpose(0, 2, 1, 3)

    q_sb, k_sb, v_sb = jnp.split(sb, 3, axis=-1)
    y_sb = stick_breaking_attention(head_rmsnorm(to_bhsd(q_sb), q_gain),
                                    head_rmsnorm(to_bhsd(k_sb), k_gain), to_bhsd(v_sb))
    y_sb = y_sb.transpose(0, 2, 1, 3).reshape(bsz, seq, SB_W)

    out = jnp.concatenate([y_rw, y_sb], axis=-1).astype(h.dtype) @ w_out
    return out, v


def odd_mixer(h, w_in, w_out, a_re, a_im, b_re, b_im, c_re, c_im, d, log_dt, w_glu, b_glu, lb, hg_gain):
    bsz, seq, _ = h.shape
    proj = (h @ w_in).astype(jnp.float32)
    cuts = [S5_W, S5_W + HG_KW, S5_W + 2 * HG_KW, S5_W + 2 * HG_KW + HG_VW]
    u, q, f_logit, i_val, gate = jnp.split(proj, cuts, axis=-1)
    y_s5 = s5_layer(u, a_re, a_im, b_re, b_im, c_re, c_im, d, log_dt, w_glu, b_glu)
    o = hgrn2_chunkwise(q, f_logit, i_val, lb.astype(jnp.float32))
    y_hg = head_rmsnorm(o, hg_gain.reshape(HG_HEADS, HG_DV)).reshape(bsz, seq, HG_VW) * jax.nn.silu(gate)
    return jnp.concatenate([y_s5, y_hg], axis=-1).astype(h.dtype) @ w_out


def setup_inputs(seed: int = 0) -> dict:
    key = jax.random.key(seed)
    keys = iter(jax.random.split(key, 64))
    f32 = jnp.float32

    def normal(shape, scale):
        return scale * jax.random.normal(next(keys), shape, f32)

    def gain(shape):
        return 1.0 + normal(shape, 0.02)

    n_vres = N_EVEN - 1
    ramp = jnp.tile(jnp.arange(RW_HD, dtype=f32) / (RW_HD - 1), RW_HEADS)
    return {
        "x": normal((BATCH, SEQ, D_MODEL), 1.0),
        "ffn1_norm": gain((DEPTH, D_MODEL)),
        "ffn1_w13": normal((DEPTH, D_MODEL, 2 * D_FF), D_MODEL ** -0.5),
        "ffn1_w2": normal((DEPTH, D_FF, D_MODEL), D_FF ** -0.5),
        "mix_norm": gain((DEPTH, D_MODEL)),
        "ffn2_norm": gain((DEPTH, D_MODEL)),
        "ffn2_w13": normal((DEPTH, D_MODEL, 2 * D_FF), D_MODEL ** -0.5),
        "ffn2_w2": normal((DEPTH, D_FF, D_MODEL), D_FF ** -0.5),
        "ev_w_in": normal((N_EVEN, D_MODEL, EVEN_IN), D_MODEL ** -0.5),
        "ev_w_out": normal((N_EVEN, EVEN_MIX, D_MODEL), EVEN_MIX ** -0.5),
        "rw_mu": jax.random.uniform(next(keys), (N_EVEN, RW_COLS), f32),
        "rw_w0": -6.0 + 7.0 * ramp + normal((N_EVEN, RW_W), 0.1),
        "rw_w2": normal((N_EVEN, W_LORA, RW_W), 0.5 * W_LORA ** -0.5),
        "rw_a0": normal((N_EVEN, RW_W), 0.1),
        "rw_a2": normal((N_EVEN, A_LORA, RW_W), 0.5 * A_LORA ** -0.5),
        "rw_g2": normal((N_EVEN, G_LORA, RW_W), G_LORA ** -0.5),
        "rw_k_k": 0.85 + normal((N_EVEN, RW_W), 0.05),
        "rw_k_a": 1.0 + normal((N_EVEN, RW_W), 0.05),
        "rw_r_k": normal((N_EVEN, RW_W), 0.1),
        "rw_ln_w": gain((N_EVEN, RW_W)),
        "rw_ln_b": normal((N_EVEN, RW_W), 0.02),
        "rw_v0": 1.0 + normal((n_vres, RW_W), 0.1),
        "rw_v1": normal((n_vres, D_MODEL, V_LORA), D_MODEL ** -0.5),
        "rw_v2": normal((n_vres, V_LORA, RW_W), V_LORA ** -0.5),
        "sb_q_gain": gain((N_EVEN, SB_HD)),
        "sb_k_gain": gain((N_EVEN, SB_HD)),
        "od_w_in": normal((N_ODD, D_MODEL, ODD_IN), D_MODEL ** -0.5),
        "od_w_out": normal((N_ODD, ODD_MIX, D_MODEL), ODD_MIX ** -0.5),
        "s5_a_re": -0.5 + normal((N_ODD, S5_G, S5_P), 0.01),
        "s5_a_im": jnp.pi * jnp.arange(S5_P, dtype=f32) + normal((N_ODD, S5_G, S5_P), 0.01),
        "s5_b_re": normal((N_ODD, S5_G, S5_P, S5_GROUP), (2 * S5_GROUP) ** -0.5),
        "s5_b_im": normal((N_ODD, S5_G, S5_P, S5_GROUP), (2 * S5_GROUP) ** -0.5),
        "s5_c_re": normal((N_ODD, S5_G, S5_GROUP, S5_P), S5_P ** -0.5),
        "s5_c_im": normal((N_ODD, S5_G, S5_GROUP, S5_P), S5_P ** -0.5),
        "s5_d": normal((N_ODD, S5_W), 1.0),
        "s5_log_dt": jax.random.uniform(next(keys), (N_ODD, S5_G), f32, math.log(DT_MIN), math.log(DT_MAX)),
        "s5_w_glu": normal((N_ODD, S5_W, S5_W), S5_W ** -0.5),
        "s5_b_glu": normal((N_ODD, S5_W), 0.02),
        "hg_lb": normal((N_ODD, HG_KW), 1.0),
        "hg_gain": gain((N_ODD, HG_VW)),
    }


def reference(x, ffn1_norm, ffn1_w13, ffn1_w2, mix_norm, ffn2_norm, ffn2_w13, ffn2_w2,
              ev_w_in, ev_w_out, rw_mu, rw_w0, rw_w2, rw_a0, rw_a2, rw_g2, rw_k_k, rw_k_a, rw_r_k,
              rw_ln_w, rw_ln_b, rw_v0, rw_v1, rw_v2, sb_q_gain, sb_k_gain,
              od_w_in, od_w_out, s5_a_re, s5_a_im, s5_b_re, s5_b_im, s5_c_re, s5_c_im, s5_d,
              s5_log_dt, s5_w_glu, s5_b_glu, hg_lb, hg_gain):
    lb_all = jnp.cumsum(jax.nn.softmax(hg_lb.astype(jnp.float32), axis=0), axis=0)
    lb_all = lb_all - lb_all[0]
    v_first = None
    for layer in range(DEPTH):
        x = x + 0.5 * swiglu_ffn(rmsnorm(x, ffn1_norm[layer]), ffn1_w13[layer], ffn1_w2[layer])
        h = rmsnorm(x, mix_norm[layer])
        if layer % 2 == 0:
            e = layer // 2
            v_mix = None if e == 0 else (rw_v0[e - 1], rw_v1[e - 1], rw_v2[e - 1])
            mixed, v = even_mixer(h, ev_w_in[e], ev_w_out[e], rw_mu[e], rw_w0[e], rw_w2[e], rw_a0[e],
                                  rw_a2[e], rw_g2[e], rw_k_k[e], rw_k_a[e], rw_r_k[e], rw_ln_w[e],
                                  rw_ln_b[e], sb_q_gain[e], sb_k_gain[e], v_first, v_mix)
            if e == 0:
                v_first = v
        else:
            o = layer // 2
            mixed = odd_mixer(h, od_w_in[o], od_w_out[o], s5_a_re[o], s5_a_im[o], s5_b_re[o], s5_b_im[o],
                              s5_c_re[o], s5_c_im[o], s5_d[o], s5_log_dt[o], s5_w_glu[o], s5_b_glu[o],
                              lb_all[o], hg_gain[o])
        x = x + mixed.astype(x.dtype)
        x = x + 0.5 * swiglu_ffn(rmsnorm(x, ffn2_norm[layer]), ffn2_w13[layer], ffn2_w2[layer])
    return x
```

```python
from concourse.bass_utils import run_bass_kernel_spmd

import contextlib
import numpy as np
import concourse.bass as bass
import concourse.mybir as mybir

F32 = mybir.dt.float32
BF16 = mybir.dt.bfloat16
AF = mybir.ActivationFunctionType
ALU = mybir.AluOpType
AX = mybir.AxisListType


class Buf:
    __slots__ = ("name", "w", "r")

    def __init__(self, name=""):
        self.name = name
        self.w = []
        self.r = {}


class Tn(Buf):
    __slots__ = ("t",)

    def __init__(self, t, name=""):
        super().__init__(name)
        self.t = t

    def __getitem__(self, idx):
        return self.t[idx]


class Eng:
    def __init__(self, name, eng, sem):
        self.name = name
        self.eng = eng
        self.sem = sem
        self.count = 0
        self.waited = {}


class K:
    def __init__(self, n_dsem=24):
        self.nc = bass.Bass("TRN2", target_bir_lowering=False)
        self.es = contextlib.ExitStack()
        nc = self.nc
        self.sems = {}
        self.engs = {}
        for nm, e in (("tensor", nc.tensor), ("vector", nc.vector), ("scalar", nc.scalar),
                      ("gpsimd", nc.gpsimd), ("sync", nc.sync)):
            s = self.es.enter_context(nc.semaphore("s_" + nm))
            self.sems["s_" + nm] = s
            self.engs[nm] = Eng(nm, e, "s_" + nm)
        self.dsems = []
        for i in range(n_dsem):
            key = "d%d" % i
            self.sems[key] = self.es.enter_context(nc.semaphore(key))
            self.dsems.append([key, 0])
        self.dnext = 0
        self.n_inst = 0
        self.uid = 0
        self.scopes = [self.es]

    def sb(self, shape, dtype=F32, name=None):
        self.uid += 1
        name = (name or "sb") + "_%d" % self.uid
        t = self.scopes[-1].enter_context(self.nc.sbuf_tensor(name, list(shape), dtype))
        return Tn(t, name)

    def ps(self, shape, dtype=F32, name=None):
        self.uid += 1
        name = (name or "ps") + "_%d" % self.uid
        t = self.scopes[-1].enter_context(self.nc.psum_tensor(name, list(shape), dtype))
        return Tn(t, name)

    @contextlib.contextmanager
    def scope(self):
        st = contextlib.ExitStack()
        self.scopes.append(st)
        try:
            yield
        finally:
            self.barrier()
            self.scopes.pop()
            st.close()

    def barrier(self):
        for E in self.engs.values():
            for F in self.engs.values():
                if F is not E and F.count > 0:
                    self._wait(E, (F.sem, F.count))
            for d in self.dsems:
                if d[1] > 0:
                    self._wait(E, (d[0], d[1]))

    def dram(self, name, shape, dtype=F32, kind="Internal"):
        t = self.nc.dram_tensor(name, list(shape), dtype, kind=kind)
        return Tn(t.ap(), name)

    def _wait(self, E, tok):
        if tok is None:
            return
        key, val = tok
        if E.waited.get(key, 0) >= val:
            return
        if key == E.sem and E.name == "tensor":
            return
        E.eng.wait_ge(self.sems[key], val)
        E.waited[key] = val
        self.n_inst += 1

    def _deps(self, E, r, w, append=False):
        for b in r:
            for t in b.w:
                self._wait(E, t)
        for b in w:
            if not append:
                for t in b.w:
                    self._wait(E, t)
            for key, val in list(b.r.items()):
                self._wait(E, (key, val))

    def _mark(self, tok, r, w, append=False):
        key, val = tok
        for b in r:
            if b.r.get(key, 0) < val:
                b.r[key] = val
        for b in w:
            if append:
                b.w = b.w + [tok]
            else:
                b.w = [tok]
            b.r = {}

    def op(self, eng, fn, r=(), w=(), signal=True, append=False):
        E = self.engs[eng]
        self._deps(E, r, w, append)
        inst = fn(E.eng)
        self.n_inst += 1
        tok = (E.sem, E.count + 1)
        if signal:
            inst.then_inc(self.sems[E.sem], 1)
            E.count += 1
        self._mark(tok, r, w, append)
        return inst

    def dma(self, out, in_, r=(), w=(), q="sync", append=False, **kw):
        E = self.engs[q]
        d = self.dsems[self.dnext]
        self.dnext = (self.dnext + 1) % len(self.dsems)
        if d[1] > 0:
            self._wait(E, (d[0], d[1]))
        self._deps(E, r, w, append)
        inst = E.eng.dma_start(out=out, in_=in_, **kw)
        d[1] += 16
        inst.then_inc(self.sems[d[0]], 16)
        self.n_inst += 1
        tok = (d[0], d[1])
        self._mark(tok, r, w, append)
        return inst

    def finish(self, bufs):
        E = self.engs["sync"]
        for b in bufs:
            for t in b.w:
                self._wait(E, t)
        for d in self.dsems:
            if d[1] > 0:
                self._wait(E, (d[0], d[1]))

    def mm(self, out, lhsT, rhs, r, w, start=True, stop=True, signal=True, **kw):
        return self.op("tensor", lambda e: e.matmul(out, lhsT, rhs, start=start, stop=stop, **kw),
                       r=r, w=w, signal=signal)

    def act(self, out, in_, func, r, w, eng="scalar", append=False, **kw):
        return self.op(eng, lambda e: e.activation(out=out, in_=in_, func=func, **kw), r=r, w=w, append=append)

    def tt(self, out, in0, in1, op, r, w, eng="vector", append=False):
        return self.op(eng, lambda e: e.tensor_tensor(out=out, in0=in0, in1=in1, op=op), r=r, w=w, append=append)

    def ts(self, out, in0, s1, s2, op0, op1, r, w, eng="vector", append=False):
        if op1 is None:
            return self.op(eng, lambda e: e.tensor_scalar(out=out, in0=in0, scalar1=s1, scalar2=None,
                                                           op0=op0), r=r, w=w, append=append)
        return self.op(eng, lambda e: e.tensor_scalar(out=out, in0=in0, scalar1=s1, scalar2=s2,
                                                       op0=op0, op1=op1), r=r, w=w, append=append)

    def stt(self, out, in0, scalar, in1, op0, op1, r, w, append=False):
        return self.op("vector", lambda e: e.scalar_tensor_tensor(out=out, in0=in0, scalar=scalar,
                                                                   in1=in1, op0=op0, op1=op1), r=r, w=w, append=append)

    def copy(self, out, in_, r, w, eng="vector", append=False):
        return self.op(eng, lambda e: e.tensor_copy(out=out, in_=in_), r=r, w=w, append=append)

    def memset(self, ap, val, w, eng="vector"):
        return self.op(eng, lambda e: e.memset(ap, val), r=(), w=w)


D = 1024
DFF = 2816
KT = D // 128
JT = DFF // 128
NT = 256
EPS = 1e-6


class DR:
    def __init__(self, k, name, shape, dtype=F32, kind="Internal"):
        self.t = k.nc.dram_tensor(name, list(shape), dtype, kind=kind).ap()
        self.bufs = {}
        self.name = name

    def b(self, key):
        if key not in self.bufs:
            self.bufs[key] = Buf("%s_%s" % (self.name, key))
        return self.bufs[key]

    def bs(self, lo, hi, g=256):
        return [self.b(i) for i in range(lo // g, (hi + g - 1) // g)]

    def __getitem__(self, idx):
        return self.t[idx]


class BankRef:
    def __init__(self, cx, tn, gen):
        self.cx, self.tn, self.gen = cx, tn, gen

    def _chk(self):
        assert self.cx.gen[self.tn.name] == self.gen, "stale PSUM bank use: " + self.tn.name

    def __getitem__(self, idx):
        self._chk()
        return self.tn.t[idx]

    @property
    def w(self):
        self._chk()
        return self.tn.w

    @w.setter
    def w(self, v):
        self.tn.w = v

    @property
    def r(self):
        self._chk()
        return self.tn.r

    @r.setter
    def r(self, v):
        self.tn.r = v


class Ctx:
    def __init__(self, k):
        self.k = k
        self.gen = {}
        self.banks = [k.ps([128, 512], F32, name="bank%d" % i) for i in range(8)]
        self.bi = 0
        self.nrot = 8
        self.ones_bf = k.sb([128, 128], BF16, name="ones_bf")
        k.memset(self.ones_bf[:], 1.0, w=[self.ones_bf])
        self.ones_f = k.sb([128, 128], F32, name="ones_f")
        k.memset(self.ones_f[:], 1.0, w=[self.ones_f])
        self.eps = k.sb([128, 1], F32, name="eps_c")
        k.memset(self.eps[:], EPS, w=[self.eps])

    def fixed(self, i):
        b = self.banks[i]
        self.gen[b.name] = self.gen.get(b.name, 0) + 1
        return BankRef(self, b, self.gen[b.name])

    def bank(self):
        self.bi = self.bi % self.nrot
        b = self.banks[self.bi]
        self.bi = (self.bi + 1) % self.nrot
        self.gen[b.name] = self.gen.get(b.name, 0) + 1
        return BankRef(self, b, self.gen[b.name])


def load_col(k, dst, vec_ap, n, r, eng_q="sync"):
    c = n // 128
    src = vec_ap.rearrange("(c p) -> p c", p=128)
    k.dma(dst[:, 0:c], src, r=r, w=[dst], q=eng_q, allow_slow_non_contiguous=True)


def rmsnorm_tile(k, cx, xt, gain, hT, sq, rstd, ntok):
    ps = cx.bank()
    k.act(sq[:, :, :ntok], xt[:, :, :ntok], AF.Square, r=[xt], w=[sq])
    for kt in range(KT):
        k.mm(ps[:, :ntok], cx.ones_bf[:], sq[:, kt, :ntok], r=[cx.ones_bf, sq], w=[ps],
             start=(kt == 0), stop=(kt == KT - 1), signal=(kt == KT - 1))
    k.act(rstd[:, :ntok], ps[:, :ntok], AF.Sqrt, r=[ps, cx.eps], w=[rstd], scale=1.0 / D, bias=cx.eps[:, 0:1])
    k.op("vector", lambda e: e.reciprocal(out=rstd[:, :ntok], in_=rstd[:, :ntok]), r=[rstd], w=[rstd])
    k.tt(hT[:, :, :ntok], xt[:, :, :ntok], rstd[:, :ntok].unsqueeze(1).to_broadcast([128, KT, ntok]), ALU.mult,
         r=[xt, rstd], w=[hT])


class FFN:
    def __init__(self, k, cx):
        self.k = k
        self.cx = cx
        self.w13b = [k.sb([128, 2 * DFF], BF16, name="w13b%d" % i) for i in range(KT)]
        self.w2b = [k.sb([128, D], BF16, name="w2b%d" % j) for j in range(JT)]
        self.xt = [k.sb([128, KT, NT], F32, name="ffn_xt%d" % i) for i in range(2)]
        self.sq = k.sb([128, KT, NT], BF16, name="ffn_sq")
        self.hT = k.sb([128, KT, NT], BF16, name="ffn_hT")
        self.actT = k.sb([128, JT, NT], BF16, name="ffn_act")
        self.rstd = k.sb([128, NT], F32, name="ffn_rstd")
        self.sg = [k.sb([128, NT], F32, name="ffn_sg%d" % i) for i in range(2)]
        self.gain = k.sb([128, KT], F32, name="ffn_gain")

    def load_weights(self, w13_ap, w2_ap, gain_ap):
        k = self.k
        CH = 1408
        for kt in range(KT):
            for c in range(2 * DFF // CH):
                k.dma(self.w13b[kt][:, c * CH:(c + 1) * CH], w13_ap[kt * 128:(kt + 1) * 128, c * CH:(c + 1) * CH],
                      r=[], w=[self.w13b[kt]], q="gpsimd", append=(c > 0))
        for j in range(JT):
            k.dma(self.w2b[j][:, :], w2_ap[j * 128:(j + 1) * 128, :], r=[], w=[self.w2b[j]], q="gpsimd")
        load_col(k, self.gain, gain_ap, D, r=[])
        for kt in range(KT):
            k.ts(self.w13b[kt][:, :], self.w13b[kt][:, :], self.gain[:, kt:kt + 1], 1.0, ALU.mult, ALU.mult,
                 r=[self.gain, self.w13b[kt]], w=[self.w13b[kt]], eng="gpsimd")

    def run(self, src, dst, TC):
        k, cx = self.k, self.cx
        ntiles = TC // NT
        for n in range(ntiles):
            t0 = n * NT
            xt = self.xt[n % 2]
            k.dma(xt[:, :, :], src[:, t0:t0 + NT].rearrange("(kt p) n -> p kt n", p=128),
                  r=src.bs(t0, t0 + NT), w=[xt])
            rmsnorm_tile(k, cx, xt, self.gain, self.hT, self.sq, self.rstd, NT)
            for j in range(JT):
                pg = cx.bank()
                pu = cx.bank()
                for kt in range(KT):
                    k.mm(pg[:, :NT], self.w13b[kt][:, j * 128:(j + 1) * 128], self.hT[:, kt, :],
                         r=[self.w13b[kt], self.hT], w=[pg], start=(kt == 0), stop=(kt == KT - 1),
                         signal=(kt == KT - 1))
                for kt in range(KT):
                    k.mm(pu[:, :NT], self.w13b[kt][:, DFF + j * 128:DFF + (j + 1) * 128], self.hT[:, kt, :],
                         r=[self.w13b[kt], self.hT], w=[pu], start=(kt == 0), stop=(kt == KT - 1),
                         signal=(kt == KT - 1))
                sg = self.sg[j % 2]
                k.act(sg[:, :], pg[:, :NT], AF.Silu, r=[pg], w=[sg])
                k.tt(self.actT[:, j, :], sg[:, :], pu[:, :NT], ALU.mult, r=[sg, pu], w=[self.actT], append=(j > 0))
            for m in range(KT):
                po = cx.bank()
                for j in range(JT):
                    k.mm(po[:, :NT], self.w2b[j][:, m * 128:(m + 1) * 128], self.actT[:, j, :],
                         r=[self.w2b[j], self.actT], w=[po], start=(j == 0), stop=(j == JT - 1),
                         signal=(j == JT - 1))
                k.stt(xt[:, m, :], po[:, :NT], 0.5, xt[:, m, :], ALU.mult, ALU.add, r=[po, xt], w=[xt])
            k.dma(dst[:, t0:t0 + NT].rearrange("(kt p) n -> p kt n", p=128), xt[:, :, :],
                  r=[xt], w=dst.bs(t0, t0 + NT))


def build_consts(k, cx):
    c = cx
    c.ident = k.sb([128, 128], F32, name="ident")
    k.op("gpsimd", lambda e: e.affine_select(out=c.ident[:], in_=c.ones_f[:], pattern=[[-1, 128]],
                                             compare_op=ALU.is_equal, fill=0.0, base=0, channel_multiplier=1),
         r=[c.ones_f], w=[c.ident])
    c.bones = k.sb([128, 128], F32, name="bones")
    k.memset(c.bones[:], 0.0, w=[c.bones])
    k.memset(c.bones[0:64, 0:64], 1.0, w=[c.bones])
    k.memset(c.bones[64:128, 64:128], 1.0, w=[c.bones])
    c.m192 = k.sb([128, 192], F32, name="m192")
    k.op("gpsimd", lambda e: e.affine_select(out=c.m192[:, 0:128], in_=c.ones_f[:], pattern=[[1, 128]],
                                             compare_op=ALU.is_gt, fill=0.0, base=0, channel_multiplier=-1),
         r=[c.ones_f], w=[c.m192])
    k.memset(c.m192[0:64, 64:128], 0.0, w=[c.m192], eng="gpsimd")
    for h in range(2):
        k.op("gpsimd", lambda e, h=h: e.affine_select(out=c.m192[h * 64:(h + 1) * 64, 128:192],
                                                      in_=c.ones_f[h * 64:(h + 1) * 64, 0:64], pattern=[[1, 64]],
                                                      compare_op=ALU.is_ge, fill=0.0, base=0, channel_multiplier=-1),
             r=[c.ones_f], w=[c.m192])
    c.msl = k.sb([128, 128], F32, name="msl")
    k.op("gpsimd", lambda e: e.affine_select(out=c.msl[:], in_=c.ones_f[:], pattern=[[-1, 128]],
                                             compare_op=ALU.is_gt, fill=0.0, base=0, channel_multiplier=1),
         r=[c.ones_f], w=[c.msl])
    k.memset(c.msl[64:128, 0:64], 0.0, w=[c.msl], eng="gpsimd")
    c.cmask = k.sb([128, 512], F32, name="cmask")
    k.memset(c.cmask[:], 1.0, w=[c.cmask])
    k.memset(c.cmask[:, :].rearrange("p (c t) -> p c t", t=64)[:, :, 0:1], 0.0, w=[c.cmask])
    c.one_c = k.sb([128, 1], F32, name="one_c")
    k.memset(c.one_c[:], 1.0, w=[c.one_c])
    c.tiny_c = k.sb([128, 1], F32, name="tiny_c")
    k.memset(c.tiny_c[:], 1e-30, w=[c.tiny_c])
    c.gneps_c = k.sb([128, 1], F32, name="gneps_c")
    k.memset(c.gneps_c[:], 64e-5, w=[c.gneps_c])


def v3(ap, c=4):
    return ap.rearrange("p (c t) -> p c t", c=c)


CW = 0.6065306597126334
RW_COLS = 1696
EVEN_IN = 3232
TT_ = 256


class EvenProj:
    def __init__(self, k, cx, first):
        self.k, self.cx, self.first = k, cx, first
        sb = k.sb
        self.Wa = [sb([128, RW_COLS], BF16, name="Wa") for _ in range(KT)]
        self.Wb = [sb([128, RW_COLS], BF16, name="Wb") for _ in range(KT)]
        self.Ws = [sb([128, 1536], BF16, name="Ws") for _ in range(KT)]
        self.gain = sb([128, KT], F32, name="mgain")
        self.wa2 = sb([64, 512], BF16, name="wa2")
        self.g2 = sb([96, 512], BF16, name="g2")
        self.v1 = sb([128, KT, 32], BF16, name="v1")
        self.v2 = sb([32, 512], BF16, name="v2")
        self.cols = {nm: sb([128, 4], F32, name="col_" + nm) for nm in
                     ("w0", "a0", "kk", "ka", "rk", "lnw", "lnb", "v0")}
        self.qg = sb([128, 1], F32, name="qg")
        self.kg = sb([128, 1], F32, name="kg")

    def alloc_act(self):
        sb = self.k.sb
        self.xt = [sb([128, KT, TT_ + 1], F32, name="m_xt") for _ in range(1)]
        self.sq = sb([128, KT, TT_ + 1], BF16, name="m_sq")
        self.hT = sb([128, KT, TT_ + 1], BF16, name="m_hT")
        self.rstd = sb([128, TT_ + 1], F32, name="m_rstd")
        self.wad = sb([64, TT_], BF16, name="wad")
        self.sgd = sb([96, TT_], BF16, name="sgd")
        self.hv1 = sb([32, TT_], BF16, name="hv1")
        P2 = range(2)
        f = lambda nm, w=TT_: [sb([128, w], F32, name=nm) for _ in P2]
        self.sgw, self.asig, self.g, self.r, self.t1, self.kkn = f("sgw"), f("asig"), f("g"), f("r"), f("t1"), f("kkn")
        self.kmod, self.bb, self.v, self.Gs, self.Gx = f("kmod"), f("bb"), f("v"), f("Gs"), f("Gx")
        self.eG, self.eGn, self.eGx, self.bon, self.yt = f("eG"), f("eGn"), f("eGx"), f("bon"), f("yt")
        self.tmp = f("tmp")
        self.vf = f("vf")
        self.AR = [sb([128, 4, 192], F32, name="AR") for _ in P2]
        self.Kb = [sb([128, 4, 128], F32, name="Kb") for _ in P2]
        self.Bb = [sb([128, 4, 128], F32, name="Bb") for _ in P2]
        self.Vb = [sb([128, 4, 128], F32, name="Vb") for _ in P2]
        self.dend = [sb([128, 4], F32, name="dend") for _ in P2]
        self.H = [sb([128, 128], F32, name="H") for _ in range(4)]
        self.Hd = [sb([128, 128], F32, name="Hd") for _ in P2]
        self.NA = [sb([128, 192], F32, name="NA") for _ in P2]
        self.KA = [sb([128, 192], F32, name="KA") for _ in P2]
        self.TOK = [sb([128, 384], F32, name="TOK") for _ in P2]
        self.U = [sb([128, 128], F32, name="U") for _ in P2]
        self.Np = [[sb([128, 128], F32, name="Np") for _ in range(6)] for _ in P2]
        self.Ap = [[sb([128, 128], F32, name="Ap") for _ in range(2)] for _ in P2]
        self.yo = [sb([128, TT_], BF16, name="yo") for _ in range(2)]
        self.qn = [sb([128, TT_], BF16, name="qn") for _ in range(2)]
        self.sqf = [sb([128, TT_], F32, name="sqf") for _ in range(2)]
        self.sd = [sb([128, TT_], F32, name="sd") for _ in range(2)]
        self.vtok = [sb([128, 512], BF16, name="vtok") for _ in range(2)]
        for T_ in self.AR + self.Kb + self.Bb + self.Vb:
            self.k.memset(T_[:, :, :], 0.0, w=[T_], eng="gpsimd")

    def load_weights(self, P, e):
        k = self.k
        w_in = P["ev_w_in"][e]
        load_col(k, self.gain, P["mix_norm"][2 * e], D, r=[])
        with k.scope():
            stage = [k.sb([128, RW_COLS], F32, name="wstage") for _ in range(2)]
            mu = k.sb([128, RW_COLS], F32, name="mu_bc")
            omu = k.sb([128, RW_COLS], F32, name="omu_bc")
            k.dma(mu[:, :], P["rw_mu"][e].partition_broadcast(128), r=[], w=[mu])
            k.ts(omu[:, :], mu[:, :], -1.0, 1.0, ALU.mult, ALU.add, r=[mu], w=[omu])
            for kt in range(KT):
                st = stage[kt % 2]
                k.dma(st[:, :], w_in[kt * 128:(kt + 1) * 128, 0:RW_COLS], r=[], w=[st])
                k.stt(self.Wa[kt][:, :], st[:, :], self.gain[:, kt:kt + 1], mu[:, :], ALU.mult, ALU.mult,
                      r=[st, self.gain, mu], w=[self.Wa[kt]])
                k.stt(self.Wb[kt][:, :], st[:, :], self.gain[:, kt:kt + 1], omu[:, :], ALU.mult, ALU.mult,
                      r=[st, self.gain, omu], w=[self.Wb[kt]])
                k.dma(self.Ws[kt][:, :], w_in[kt * 128:(kt + 1) * 128, RW_COLS:EVEN_IN], r=[], w=[self.Ws[kt]],
                      q="gpsimd")
                k.ts(self.Ws[kt][:, :], self.Ws[kt][:, :], self.gain[:, kt:kt + 1], 1.0, ALU.mult, ALU.mult,
                     r=[self.gain, self.Ws[kt]], w=[self.Ws[kt]], eng="gpsimd")
        k.dma(self.wa2[0:32, :], P["rw_w2"][e], r=[], w=[self.wa2], q="gpsimd")
        k.dma(self.wa2[32:64, :], P["rw_a2"][e], r=[], w=[self.wa2], q="gpsimd", append=True)
        k.dma(self.g2[:, :], P["rw_g2"][e], r=[], w=[self.g2], q="gpsimd")
        if not self.first:
            k.dma(self.v1[:, :, :], P["rw_v1"][e - 1].rearrange("(kt p) c -> p kt c", p=128), r=[], w=[self.v1],
                  q="gpsimd")
            for kt in range(KT):
                k.ts(self.v1[:, kt, :], self.v1[:, kt, :], self.gain[:, kt:kt + 1], 1.0, ALU.mult, ALU.mult,
                     r=[self.gain, self.v1], w=[self.v1], eng="gpsimd")
            k.dma(self.v2[:, :], P["rw_v2"][e - 1], r=[], w=[self.v2], q="gpsimd")
            load_col(k, self.cols["v0"], P["rw_v0"][e - 1], 512, r=[])
        for nm, key in (("w0", "rw_w0"), ("a0", "rw_a0"), ("kk", "rw_k_k"), ("ka", "rw_k_a"), ("rk", "rw_r_k"),
                        ("lnw", "rw_ln_w"), ("lnb", "rw_ln_b")):
            load_col(k, self.cols[nm], P[key][e], 512, r=[])
        for h in range(2):
            k.dma(self.qg[h * 64:(h + 1) * 64, 0:1], P["sb_q_gain"][e].rearrange("(p o) -> p o", o=1), r=[],
                  w=[self.qg], append=(h > 0))
            k.dma(self.kg[h * 64:(h + 1) * 64, 0:1], P["sb_k_gain"][e].rearrange("(p o) -> p o", o=1), r=[],
                  w=[self.kg], append=(h > 0))
        k.ts(self.qg[:, :], self.qg[:, :], 0.125, None, ALU.mult, None, r=[self.qg], w=[self.qg])
        self.alloc_act()

    def proj(self, ps, Wlist, c0, c1, lo, ntok, first=True, last=True):
        k = self.k
        for kt in range(KT):
            k.mm(ps[0:c1 - c0, :ntok], Wlist[kt][:, c0:c1], self.hT[:, kt, lo:lo + ntok],
                 r=[Wlist[kt], self.hT], w=[ps], start=(first and kt == 0), stop=(last and kt == KT - 1),
                 signal=(last and kt == KT - 1))

    def proj_rw(self, ps, c0, c1):
        self.proj(ps, self.Wb, c0, c1, 1, TT_, True, False)
        self.proj(ps, self.Wa, c0, c1, 0, TT_, False, True)

    def run(self, src, S, TC, vfirst, sbq, sbk, sbv, mixT):
        k, cx = self.k, self.cx
        N = TT_
        for n in range(TC // N):
            t0 = n * N
            seq_start = (t0 % S == 0)
            xt = self.xt[n % len(self.xt)]
            if seq_start:
                k.memset(xt[:, :, 0:1], 0.0, w=[xt])
                k.dma(xt[:, :, 1:N + 1], src[:, t0:t0 + N].rearrange("(kt p) n -> p kt n", p=128),
                      r=src.bs(t0, t0 + N), w=[xt], append=True)
                for p in range(4):
                    k.memset(self.H[p][:, :], 0.0, w=[self.H[p]])
            else:
                k.dma(xt[:, :, :], src[:, t0 - 1:t0 + N].rearrange("(kt p) n -> p kt n", p=128),
                      r=src.bs(t0 - 1, t0 + N), w=[xt])
            rmsnorm_tile(k, cx, xt, None, self.hT, self.sq, self.rstd, N + 1)
            ps = cx.bank()
            self.proj_rw(ps, 1536, 1600)
            k.act(self.wad[0:32, :], ps[0:32, :N], AF.Tanh, r=[ps], w=[self.wad])
            k.act(self.wad[32:64, :], ps[32:64, :N], AF.Copy, r=[ps], w=[self.wad], append=True)
            ps = cx.bank()
            self.proj_rw(ps, 1600, 1696)
            k.act(self.sgd[:, :], ps[0:96, :N], AF.Sigmoid, r=[ps], w=[self.sgd])
            if not self.first:
                ps = cx.bank()
                for kt in range(KT):
                    k.mm(ps[0:32, :N], self.v1[:, kt, :], self.hT[:, kt, 1:N + 1], r=[self.v1, self.hT], w=[ps],
                         start=(kt == 0), stop=(kt == KT - 1), signal=(kt == KT - 1))
                k.act(self.hv1[:, :], ps[0:32, :N], AF.Copy, r=[ps], w=[self.hv1])
            for half in range(2):
                for q in range(2):
                    self.prep(2 * half + q, q, n, t0, vfirst)
                for c in range(4):
                    self.chunk(c, half)
                for q in range(2):
                    self.post(2 * half + q, q, n, t0, mixT)
            self.sb_prep(n, t0, sbq, sbk, sbv)

    def prep(self, p, q, n, t0, vfirst):
        k, cx = self.k, self.cx
        cols = self.cols
        N = TT_
        cs = slice(p * 128, (p + 1) * 128)
        col = lambda nm: cols[nm][:, p:p + 1]
        ps_r, ps_k, ps_v = cx.bank(), cx.bank(), cx.bank()
        self.proj_rw(ps_r, p * 128, (p + 1) * 128)
        self.proj_rw(ps_k, 512 + p * 128, 512 + (p + 1) * 128)
        self.proj_rw(ps_v, 1024 + p * 128, 1024 + (p + 1) * 128)
        r_, t1, kkn, kmod, bb, v_ = self.r[q], self.t1[q], self.kkn[q], self.kmod[q], self.bb[q], self.v[q]
        sgw, asig, g_, tmp = self.sgw[q], self.asig[q], self.g[q], self.tmp[q]
        k.act(r_[:, :], ps_r[:, :N], AF.Copy, r=[ps_r], w=[r_])
        k.ts(t1[:, :], ps_k[:, :N], col("kk"), None, ALU.mult, None, r=[ps_k, cols["kk"]], w=[t1])
        ps_z = cx.bank()
        k.mm(ps_z[:, :N], self.wa2[32:64, cs], self.wad[32:64, :], r=[self.wa2, self.wad], w=[ps_z])
        k.act(asig[:, :], ps_z[:, :N], AF.Sigmoid, r=[ps_z, cols["a0"]], w=[asig], bias=col("a0"))
        k.ts(tmp[:, :], asig[:, :], -1.0, col("ka"), ALU.add, ALU.mult, r=[asig, cols["ka"]], w=[tmp])
        k.stt(kmod[:, :], tmp[:, :], 1.0, ps_k[:, :N], ALU.add, ALU.mult, r=[tmp, ps_k], w=[kmod])
        vdst = vfirst.b(("p", p, n))
        if self.first:
            k.act(v_[:, :], ps_v[:, :N], AF.Copy, r=[ps_v], w=[v_])
            k.dma(vfirst[p * 128:(p + 1) * 128, t0:t0 + N], v_[:, :], r=[v_], w=[vdst])
        else:
            vf = self.vf[q]
            k.dma(vf[:, :], vfirst[p * 128:(p + 1) * 128, t0:t0 + N], r=[vdst], w=[vf])
            ps_m = cx.bank()
            k.mm(ps_m[:, :N], self.v2[0:32, cs], self.hv1[0:32, :], r=[self.v2, self.hv1], w=[ps_m])
            k.act(tmp[:, :], ps_m[:, :N], AF.Sigmoid, r=[ps_m, cols["v0"]], w=[tmp], bias=col("v0"))
            k.tt(vf[:, :], vf[:, :], ps_v[:, :N], ALU.subtract, r=[vf, ps_v], w=[vf])
            k.tt(vf[:, :], vf[:, :], tmp[:, :], ALU.mult, r=[vf, tmp], w=[vf])
            k.tt(v_[:, :], vf[:, :], ps_v[:, :N], ALU.add, r=[vf, ps_v], w=[v_])
        ps_z = cx.bank()
        k.mm(ps_z[:, :N], self.wa2[0:32, cs], self.wad[0:32, :], r=[self.wa2, self.wad], w=[ps_z])
        k.act(sgw[:, :], ps_z[:, :N], AF.Sigmoid, r=[ps_z, cols["w0"]], w=[sgw], bias=col("w0"))
        ps_z = cx.bank()
        k.mm(ps_z[:, :N], self.g2[0:96, cs], self.sgd[0:96, :], r=[self.g2, self.sgd], w=[ps_z])
        k.act(g_[:, :], ps_z[:, :N], AF.Copy, r=[ps_z], w=[g_])
        k.act(tmp[:, :], t1[:, :], AF.Square, r=[t1], w=[tmp])
        ps_s = cx.bank()
        k.mm(ps_s[:, :N], cx.bones[:, :], tmp[:, :], r=[cx.bones, tmp], w=[ps_s])
        k.act(kkn[:, :], ps_s[:, :N], AF.Sqrt, r=[ps_s, cx.tiny_c], w=[kkn], bias=cx.tiny_c[:, 0:1])
        k.op("vector", lambda e: e.reciprocal(out=kkn[:, :], in_=kkn[:, :]), r=[kkn], w=[kkn])
        k.tt(kkn[:, :], kkn[:, :], t1[:, :], ALU.mult, r=[kkn, t1], w=[kkn])
        k.tt(bb[:, :], kkn[:, :], asig[:, :], ALU.mult, r=[kkn, asig], w=[bb])
        Gs, Gx, eG, eGn, eGx = self.Gs[q], self.Gx[q], self.eG[q], self.eGn[q], self.eGx[q]
        k.op("vector", lambda e: e.tensor_tensor_scan(out=Gs[:, :], data0=cx.cmask[:, :N], data1=sgw[:, :],
                                                      initial=0.0, op0=ALU.mult, op1=ALU.add),
             r=[cx.cmask, sgw], w=[Gs])
        k.tt(Gx[:, :], Gs[:, :], sgw[:, :], ALU.subtract, r=[Gs, sgw], w=[Gx])
        k.act(eG[:, :], Gs[:, :], AF.Exp, r=[Gs], w=[eG], scale=-CW)
        k.act(eGn[:, :], Gs[:, :], AF.Exp, r=[Gs], w=[eGn], scale=CW)
        k.act(eGx[:, :], Gx[:, :], AF.Exp, r=[Gx], w=[eGx], scale=-CW)
        AR, Kb, Bb, Vb = self.AR[q], self.Kb[q], self.Bb[q], self.Vb[q]
        for h in range(2):
            hs = slice(h * 64, (h + 1) * 64)
            ap_ = (h > 0)
            k.stt(AR[hs, :, h * 64:(h + 1) * 64], v3(kkn[hs, :]), -1.0, v3(eGx[hs, :]), ALU.mult, ALU.mult,
                  r=[kkn, eGx], w=[AR], append=ap_)
            k.tt(Kb[hs, :, h * 64:(h + 1) * 64], v3(kmod[hs, :]), v3(eGn[hs, :]), ALU.mult,
                 r=[kmod, eGn], w=[Kb], append=ap_)
            k.tt(Bb[hs, :, h * 64:(h + 1) * 64], v3(bb[hs, :]), v3(eGn[hs, :]), ALU.mult,
                 r=[bb, eGn], w=[Bb], append=ap_, eng="gpsimd")
            k.copy(Vb[hs, :, h * 64:(h + 1) * 64], v3(v_[hs, :]), r=[v_], w=[Vb], append=ap_, eng="gpsimd")
        k.tt(AR[:, :, 128:192], v3(r_[:, :]), v3(eG[:, :]), ALU.mult, r=[r_, eG], w=[AR], append=True)
        k.copy(self.dend[q][:, :], v3(eG[:, :])[:, :, 63], r=[eG], w=[self.dend[q]])
        k.stt(tmp[:, :], r_[:, :], col("rk"), kmod[:, :], ALU.mult, ALU.mult, r=[r_, cols["rk"], kmod], w=[tmp])
        ps_s = cx.bank()
        k.mm(ps_s[:, :N], cx.bones[:, :], tmp[:, :], r=[cx.bones, tmp], w=[ps_s])
        k.tt(self.bon[q][:, :], v_[:, :], ps_s[:, :N], ALU.mult, r=[v_, ps_s], w=[self.bon[q]])

    def chunk(self, c, half):
        k, cx = self.k, self.cx
        Q2 = range(2)
        Hs = [self.H[2 * half + q] for q in Q2]
        for q in Q2:
            AR, Kb, Bb, Vb = self.AR[q], self.Kb[q], self.Bb[q], self.Vb[q]
            pA = cx.bank()
            k.mm(pA[:, 0:192], Bb[:, c, :], AR[:, c, :], r=[Bb, AR], w=[pA])
            k.tt(self.NA[q][:, :], pA[:, 0:192], cx.m192[:, :], ALU.mult, r=[pA, cx.m192], w=[self.NA[q]])
            pB = cx.bank()
            k.mm(pB[:, 0:192], Kb[:, c, :], AR[:, c, :], r=[Kb, AR], w=[pB])
            k.tt(self.KA[q][:, :], pB[:, 0:192], cx.m192[:, :], ALU.mult, r=[pB, cx.m192], w=[self.KA[q]])
            pC = cx.bank()
            k.mm(pC[:, 0:128], AR[:, c, 0:128], Bb[:, c, :], r=[AR, Bb], w=[pC])
            k.tt(self.Ap[q][0][:, :], pC[:, 0:128], cx.msl[:, :], ALU.mult, r=[pC, cx.msl], w=[self.Ap[q][0]])
            k.copy(self.Np[q][0][:, :], self.NA[q][:, 0:128], r=[self.NA[q]], w=[self.Np[q][0]], eng="gpsimd")
            pT = cx.bank()
            for i, X in enumerate((Kb, Bb, Vb)):
                k.op("tensor", lambda e, X=X, i=i, pT=pT: e.transpose(pT[:, i * 128:(i + 1) * 128], X[:, c, :],
                                                                       cx.ident[:, :]),
                     r=[X, cx.ident], w=[pT], signal=(i == 2))
            k.act(self.TOK[q][:, :], pT[:, 0:384], AF.Copy, r=[pT], w=[self.TOK[q]])
        for q in Q2:
            pw = cx.bank()
            k.mm(pw[:, 0:128], self.AR[q][:, c, 0:128], Hs[q][:, :], r=[self.AR[q], Hs[q]], w=[pw],
                 start=True, stop=False, signal=False)
            k.mm(pw[:, 0:128], self.KA[q][:, 0:128], self.TOK[q][:, 256:384], r=[self.KA[q], self.TOK[q]], w=[pw],
                 start=False, stop=True)
            k.copy(self.U[q][:, :], pw[:, 0:128], r=[pw], w=[self.U[q]])
            k.ts(self.Hd[q][:, :], Hs[q][:, :], self.dend[q][:, c:c + 1], None, ALU.mult, None,
                 r=[Hs[q], self.dend[q]], w=[self.Hd[q]], eng="gpsimd")
        for j in range(5):
            for q in Q2:
                Np, Ap = self.Np[q], self.Ap[q]
                pn = cx.bank()
                k.mm(pn[:, 0:128], Ap[j % 2][:, :], Np[j][:, :], r=[Ap[j % 2], Np[j]], w=[pn])
                k.act(Np[j + 1][:, :], pn[:, 0:128], AF.Copy, r=[pn], w=[Np[j + 1]])
                if j < 4:
                    pa = cx.bank()
                    k.mm(pa[:, 0:128], Np[j][:, :], Ap[j % 2][:, :], r=[Np[j], Ap[j % 2]], w=[pa])
                    k.copy(Ap[(j + 1) % 2][:, :], pa[:, 0:128], r=[pa], w=[Ap[(j + 1) % 2]], eng="vector")
        for j in range(6):
            for q in Q2:
                pu = cx.bank()
                k.mm(pu[:, 0:128], self.Np[q][j][:, :], self.U[q][:, :], r=[self.Np[q][j], self.U[q]], w=[pu])
                k.tt(self.U[q][:, :], self.U[q][:, :], pu[:, 0:128], ALU.add, r=[self.U[q], pu], w=[self.U[q]])
        for q in Q2:
            py = cx.bank()
            k.mm(py[:, 0:64], Hs[q][:, :], self.AR[q][:, c, 128:192], r=[Hs[q], self.AR[q]], w=[py],
                 start=True, stop=False, signal=False)
            k.mm(py[:, 0:64], self.TOK[q][:, 256:384], self.KA[q][:, 128:192], r=[self.TOK[q], self.KA[q]], w=[py],
                 start=False, stop=False, signal=False)
            k.mm(py[:, 0:64], self.U[q][:, :], self.NA[q][:, 128:192], r=[self.U[q], self.NA[q]], w=[py],
                 start=False, stop=True)
            k.act(self.yt[q][:, c * 64:(c + 1) * 64], py[:, 0:64], AF.Copy, r=[py], w=[self.yt[q]], append=(c > 0))
            ph = cx.bank()
            k.mm(ph[:, 0:128], self.TOK[q][:, 0:128], self.TOK[q][:, 256:384], r=[self.TOK[q]], w=[ph],
                 start=True, stop=False, signal=False)
            k.mm(ph[:, 0:128], self.TOK[q][:, 128:256], self.U[q][:, :], r=[self.TOK[q], self.U[q]], w=[ph],
                 start=False, stop=True)
            k.stt(Hs[q][:, :], ph[:, 0:128], self.dend[q][:, c:c + 1], self.Hd[q][:, :], ALU.mult, ALU.add,
                  r=[ph, self.dend[q], self.Hd[q]], w=[Hs[q]])

    def post(self, p, q, n, t0, mixT):
        k, cx = self.k, self.cx
        N = TT_
        yt, tmp, g_, bon = self.yt[q], self.tmp[q], self.g[q], self.bon[q]
        col = lambda nm: self.cols[nm][:, p:p + 1]
        pm = cx.bank()
        k.mm(pm[:, :N], cx.bones[:, :], yt[:, :], r=[cx.bones, yt], w=[pm])
        k.stt(yt[:, :], pm[:, :N], -1.0 / 64, yt[:, :], ALU.mult, ALU.add, r=[pm, yt], w=[yt])
        k.act(tmp[:, :], yt[:, :], AF.Square, r=[yt], w=[tmp])
        pv = cx.bank()
        k.mm(pv[:, :N], cx.bones[:, :], tmp[:, :], r=[cx.bones, tmp], w=[pv])
        k.act(tmp[:, :], pv[:, :N], AF.Sqrt, r=[pv, cx.gneps_c], w=[tmp], scale=1.0 / 64, bias=cx.gneps_c[:, 0:1])
        k.op("vector", lambda e: e.reciprocal(out=tmp[:, :], in_=tmp[:, :]), r=[tmp], w=[tmp])
        k.tt(yt[:, :], yt[:, :], tmp[:, :], ALU.mult, r=[yt, tmp], w=[yt])
        k.ts(yt[:, :], yt[:, :], col("lnw"), col("lnb"), ALU.mult, ALU.add, r=[yt, self.cols["lnw"], self.cols["lnb"]],
             w=[yt])
        k.tt(yt[:, :], yt[:, :], bon[:, :], ALU.add, r=[yt, bon], w=[yt])
        yo = self.yo[q]
        k.tt(yo[:, :], yt[:, :], g_[:, :], ALU.mult, r=[yt, g_], w=[yo])
        k.dma(mixT[p * 128:(p + 1) * 128, t0:t0 + N], yo[:, :], r=[yo], w=[mixT.b(("rw", p, n))])

    def sb_prep(self, n, t0, sbq, sbk, sbv):
        k, cx = self.k, self.cx
        N = TT_
        for i in range(8):
            ps = cx.bank()
            self.proj(ps, self.Ws, i * 128, (i + 1) * 128, 1, N)
            sqf, sd, qn = self.sqf[i % 2], self.sd[i % 2], self.qn[i % 2]
            k.act(sqf[:, :], ps[:, :N], AF.Square, r=[ps], w=[sqf])
            pss = cx.bank()
            k.mm(pss[:, :N], cx.bones[:, :], sqf[:, :], r=[cx.bones, sqf], w=[pss])
            k.act(sd[:, :], pss[:, :N], AF.Sqrt, r=[pss, cx.eps], w=[sd], scale=1.0 / 64, bias=cx.eps[:, 0:1])
            k.op("vector", lambda e, sd=sd: e.reciprocal(out=sd[:, :], in_=sd[:, :]), r=[sd], w=[sd])
            gcol = self.qg if i < 4 else self.kg
            k.stt(qn[:, :], ps[:, :N], gcol[:, 0:1], sd[:, :], ALU.mult, ALU.mult, r=[ps, gcol, sd], w=[qn])
            dst = sbq if i < 4 else sbk
            j = i % 4
            k.dma(dst[j * 128:(j + 1) * 128, t0:t0 + N], qn[:, :], r=[qn], w=[dst.b((j, n))])
        for blk in range(N // 128):
            ps = cx.bank()
            for kt in range(KT):
                k.mm(ps[:, 0:512], self.hT[:, kt, 1 + blk * 128:1 + (blk + 1) * 128], self.Ws[kt][:, 1024:1536],
                     r=[self.hT, self.Ws[kt]], w=[ps], start=(kt == 0), stop=(kt == KT - 1), signal=(kt == KT - 1))
            vt = self.vtok[blk % 2]
            k.act(vt[:, :], ps[:, 0:512], AF.Copy, r=[ps], w=[vt])
            k.dma(sbv[t0 + blk * 128:t0 + (blk + 1) * 128, :], vt[:, :], r=[vt], w=[sbv.b((n, blk))])


class SBAttn:
    def __init__(self, k, cx, S):
        self.k, self.cx, self.S = k, cx, S
        sb = k.sb
        self.Kt = sb([64, S], BF16, name="sbK")
        self.Vt = sb([128, S // 128, 64], BF16, name="sbV")
        self.Q = [sb([64, 512], BF16, name="sbQ") for _ in range(2)]
        self.e = [sb([128, 512], F32, name="sb_e") for _ in range(2)]
        self.P = [sb([128, 512], F32, name="sb_P") for _ in range(2)]
        self.u = [sb([128, 512], F32, name="sb_u") for _ in range(2)]
        self.wg = [sb([128, 512], BF16, name="sb_w") for _ in range(2)]
        self.Pacc = sb([128, 512], F32, name="sb_Pacc")
        self.mask = sb([128, 4, 512], F32, name="sb_mask")
        self.triu = sb([128, 128], F32, name="sb_triu")
        self.osb = [sb([64, 512], BF16, name="sb_o") for _ in range(2)]
        for rel in range(4):
            k.op("gpsimd", lambda e, rel=rel: e.affine_select(out=self.mask[:, rel, :],
                                                              in_=cx.ones_f[:, 0:1].to_broadcast([128, 512]),
                                                              pattern=[[1, 512]], compare_op=ALU.is_gt, fill=0.0,
                                                              base=-rel * 128, channel_multiplier=-1),
                 r=[cx.ones_f], w=[self.mask], append=(rel > 0))
        k.op("gpsimd", lambda e: e.affine_select(out=self.triu[:, :], in_=cx.ones_f[:, :], pattern=[[-1, 128]],
                                                 compare_op=ALU.is_gt, fill=0.0, base=0, channel_multiplier=1),
             r=[cx.ones_f], w=[self.triu])

    def run(self, TC, sbq, sbk, sbv, mixT):
        k, cx, S = self.k, self.cx, self.S
        cx.nrot = 7
        it = 0
        for b in range(TC // S):
            for h in range(8):
                k.dma(self.Kt[:, :], sbk[h * 64:(h + 1) * 64, b * S:(b + 1) * S], r=[], w=[self.Kt])
                k.dma(self.Vt[:, :, :], sbv[b * S:(b + 1) * S, h * 64:(h + 1) * 64].rearrange("(blk p) d -> p blk d", p=128),
                      r=[], w=[self.Vt])
                for qt in range(S // 512):
                    Q = self.Q[qt % 2]
                    c0 = b * S + qt * 512
                    k.dma(Q[:, :], sbq[h * 64:(h + 1) * 64, c0:c0 + 512], r=[], w=[Q])
                    po = cx.fixed(7)
                    nkb = (qt + 1) * 4
                    for idx, kb in enumerate(range(nkb - 1, -1, -1)):
                        rel = kb - qt * 4
                        e_, P_, u_, wg = self.e[it % 2], self.P[it % 2], self.u[it % 2], self.wg[it % 2]
                        it += 1
                        pz = cx.bank()
                        k.mm(pz[:, :512], self.Kt[:, kb * 128:(kb + 1) * 128], Q[:, :], r=[self.Kt, Q], w=[pz])
                        k.act(e_[:, :], pz[:, :512], AF.Exp, r=[pz], w=[e_])
                        k.act(P_[:, :], e_[:, :], AF.Ln, r=[e_, cx.one_c], w=[P_], bias=cx.one_c[:, 0:1])
                        if rel >= 0:
                            k.tt(P_[:, :], P_[:, :], self.mask[:, rel, :], ALU.mult, r=[P_, self.mask], w=[P_],
                                 eng="gpsimd")
                        pst = cx.bank()
                        k.mm(pst[:, :512], self.triu[:, :], P_[:, :], r=[self.triu, P_], w=[pst], start=True,
                             stop=(idx == 0), signal=(idx == 0))
                        if idx > 0:
                            k.mm(pst[:, :512], cx.ones_f[:, :], self.Pacc[:, :], r=[cx.ones_f, self.Pacc], w=[pst],
                                 start=False, stop=True)
                        k.tt(u_[:, :], pz[:, :512], P_[:, :], ALU.subtract, r=[pz, P_], w=[u_])
                        k.tt(u_[:, :], u_[:, :], pst[:, :512], ALU.subtract, r=[u_, pst], w=[u_])
                        k.act(wg[:, :], u_[:, :], AF.Exp, r=[u_], w=[wg])
                        if rel >= 0:
                            k.tt(wg[:, :], wg[:, :], self.mask[:, rel, :], ALU.mult, r=[wg, self.mask], w=[wg])
                        k.mm(po[0:64, :512], self.Vt[:, kb, :], wg[:, :], r=[self.Vt, wg], w=[po],
                             start=(idx == 0), stop=(idx == nkb - 1), signal=True)
                        if idx == 0:
                            k.copy(self.Pacc[:, :], P_[:, :], r=[P_], w=[self.Pacc], eng="gpsimd")
                        elif idx < nkb - 1:
                            k.tt(self.Pacc[:, :], self.Pacc[:, :], P_[:, :], ALU.add, r=[self.Pacc, P_], w=[self.Pacc],
                                 eng="gpsimd")
                    osb = self.osb[qt % 2]
                    k.act(osb[:, :], po[0:64, :512], AF.Copy, r=[po], w=[osb])
                    k.dma(mixT[512 + h * 64:512 + (h + 1) * 64, c0:c0 + 512], osb[:, :], r=[osb],
                          w=[mixT.b(("sb", b, h, qt))])
        cx.nrot = 8


class OutProj:
    def __init__(self, k, cx):
        self.k, self.cx = k, cx
        self.Wo = [k.sb([128, D], BF16, name="Wo") for _ in range(KT)]
        self.xt = [k.sb([128, KT, NT], F32, name="o_xt") for _ in range(2)]
        self.mt = [k.sb([128, KT, NT], BF16, name="o_mt") for _ in range(2)]

    def load_weights(self, w_out_ap):
        for kt in range(KT):
            self.k.dma(self.Wo[kt][:, :], w_out_ap[kt * 128:(kt + 1) * 128, :], r=[], w=[self.Wo[kt]], q="gpsimd")

    def run(self, src, dst, mixT, TC):
        k, cx = self.k, self.cx
        for n in range(TC // NT):
            t0 = n * NT
            xt, mt = self.xt[n % 2], self.mt[n % 2]
            k.dma(xt[:, :, :], src[:, t0:t0 + NT].rearrange("(kt p) n -> p kt n", p=128), r=src.bs(t0, t0 + NT), w=[xt])
            k.dma(mt[:, :, :], mixT[:, t0:t0 + NT].rearrange("(kt p) n -> p kt n", p=128), r=[], w=[mt])
            for m in range(KT):
                po = cx.bank()
                for kt in range(KT):
                    k.mm(po[:, :NT], self.Wo[kt][:, m * 128:(m + 1) * 128], mt[:, kt, :], r=[self.Wo[kt], mt], w=[po],
                         start=(kt == 0), stop=(kt == KT - 1), signal=(kt == KT - 1))
                k.tt(xt[:, m, :], xt[:, m, :], po[:, :NT], ALU.add, r=[xt, po], w=[xt])
            k.dma(dst[:, t0:t0 + NT].rearrange("(kt p) n -> p kt n", p=128), xt[:, :, :], r=[xt], w=dst.bs(t0, t0 + NT))


def even_mixer_layer(k, cx, P, e, src, dst, S, TC, scr):
    with k.scope():
        ep = EvenProj(k, cx, first=(e == 0))
        ep.load_weights(P, e)
        ep.run(src, S, TC, scr["vfirst"], scr["sbq"], scr["sbk"], scr["sbv"], scr["mixT"])
    with k.scope():
        sa = SBAttn(k, cx, S)
        sa.run(TC, scr["sbq"], scr["sbk"], scr["sbv"], scr["mixT"])
    with k.scope():
        op = OutProj(k, cx)
        op.load_weights(P["ev_w_out"][e])
        op.run(src, dst, scr["mixT"], TC)


ODD_IN = 3328
PI = 3.141592653589793


class OddProj:
    def __init__(self, k, cx, o):
        self.k, self.cx, self.o = k, cx, o
        sb = k.sb
        self.W = [sb([128, ODD_IN], BF16, name="Wodd") for _ in range(KT)]
        self.gain = sb([128, KT], F32, name="ogain")
        self.lb = sb([128, 6], F32, name="lb")
        self.oml = sb([128, 6], F32, name="oml")
        self.l0 = sb([128, 6], F32, name="l0")
        self.l1 = sb([128, 6], F32, name="l1")
        self.hgg = sb([128, 6], F32, name="hgg")

    def load_weights(self, P):
        k, o = self.k, self.o
        load_col(k, self.gain, P["mix_norm"][2 * o + 1], D, r=[])
        w_in = P["od_w_in"][o]
        CH = 1664
        for kt in range(KT):
            for c in range(2):
                k.dma(self.W[kt][:, c * CH:(c + 1) * CH], w_in[kt * 128:(kt + 1) * 128, c * CH:(c + 1) * CH], r=[],
                      w=[self.W[kt]], q="gpsimd", append=(c > 0))
            k.ts(self.W[kt][:, :], self.W[kt][:, :], self.gain[:, kt:kt + 1], 1.0, ALU.mult, ALU.mult,
                 r=[self.gain, self.W[kt]], w=[self.W[kt]], eng="gpsimd")
        load_col(k, self.hgg, P["hg_gain"][o], 768, r=[])
        if o == 0:
            k.memset(self.lb[:, :], 0.0, w=[self.lb])
        else:
            load_col(k, self.l0, P["hg_lb"][0], 768, r=[])
            load_col(k, self.l1, P["hg_lb"][1], 768, r=[])
            k.tt(self.l1[:, :], self.l1[:, :], self.l0[:, :], ALU.subtract, r=[self.l1, self.l0], w=[self.l1])
            k.act(self.lb[:, :], self.l1[:, :], AF.Sigmoid, r=[self.l1], w=[self.lb])
        k.ts(self.oml[:, :], self.lb[:, :], -1.0, 1.0, ALU.mult, ALU.add, r=[self.lb], w=[self.oml])
        sb = k.sb
        N = TT_
        self.xt = sb([128, KT, N], F32, name="od_xt")
        self.sq = sb([128, KT, N], BF16, name="od_sq")
        self.hT = sb([128, KT, N], BF16, name="od_hT")
        self.rstd = sb([128, N], F32, name="od_rstd")
        self.us = [sb([128, N], F32, name="od_u") for _ in range(2)]
        H6 = range(6)
        f = lambda nm: [sb([128, N], F32, name=nm) for _ in H6]
        self.Qt, self.Kt, self.It, self.sg, self.ot = f("Qt"), f("Kt"), f("It"), f("sgate"), f("ot")
        self.fs = [sb([128, N], F32, name="fs") for _ in range(2)]
        self.G = [sb([128, N], F32, name="Gh") for _ in range(2)]
        self.eG = [sb([128, N], F32, name="eGh") for _ in range(2)]
        self.eGn = [sb([128, N], F32, name="eGnh") for _ in range(2)]
        self.dend = [sb([128, 4], F32, name="dendh") for _ in H6]
        self.H = [sb([128, 128], F32, name="Hh") for _ in H6]
        self.Hd = [sb([128, 128], F32, name="Hdh") for _ in H6]
        self.AttT = [sb([64, 64], F32, name="AttT") for _ in H6]
        self.TOK = [sb([64, 256], F32, name="TOKh") for _ in H6]
        self.yo = [sb([128, N], BF16, name="yoh") for _ in range(2)]
        self.tmp = [sb([128, N], F32, name="tmph") for _ in range(2)]

    def proj(self, ps, c0, c1):
        k = self.k
        for kt in range(KT):
            k.mm(ps[0:c1 - c0, :TT_], self.W[kt][:, c0:c1], self.hT[:, kt, :], r=[self.W[kt], self.hT], w=[ps],
                 start=(kt == 0), stop=(kt == KT - 1), signal=(kt == KT - 1))

    def run(self, src, S, TC, s5u, mixT):
        k, cx = self.k, self.cx
        N = TT_
        for n in range(TC // N):
            t0 = n * N
            xt = self.xt
            k.dma(xt[:, :, :], src[:, t0:t0 + N].rearrange("(kt p) n -> p kt n", p=128), r=src.bs(t0, t0 + N), w=[xt])
            if t0 % S == 0:
                for hd in range(6):
                    k.memset(self.H[hd][:, :], 0.0, w=[self.H[hd]])
            rmsnorm_tile(k, cx, xt, None, self.hT, self.sq, self.rstd, N)
            for i in range(2):
                ps = cx.bank()
                self.proj(ps, i * 128, (i + 1) * 128)
                us = self.us[i]
                k.act(us[:, :], ps[:, :N], AF.Copy, r=[ps], w=[us])
                k.dma(s5u[i * 128:(i + 1) * 128, t0:t0 + N], us[:, :], r=[us], w=[s5u.b((i, n))])
            for hd in range(6):
                q2 = hd % 2
                ps_q, ps_f, ps_i, ps_g = cx.bank(), cx.bank(), cx.bank(), cx.bank()
                self.proj(ps_f, 1024 + hd * 128, 1024 + (hd + 1) * 128)
                self.proj(ps_q, 256 + hd * 128, 256 + (hd + 1) * 128)
                self.proj(ps_i, 1792 + hd * 128, 1792 + (hd + 1) * 128)
                self.proj(ps_g, 2560 + hd * 128, 2560 + (hd + 1) * 128)
                fs, G, eG, eGn = self.fs[q2], self.G[q2], self.eG[q2], self.eGn[q2]
                k.act(fs[:, :], ps_f[:, :N], AF.Sigmoid, r=[ps_f], w=[fs])
                k.ts(fs[:, :], fs[:, :], self.oml[:, hd:hd + 1], self.lb[:, hd:hd + 1], ALU.mult, ALU.add,
                     r=[fs, self.oml, self.lb], w=[fs])
                k.act(G[:, :], fs[:, :], AF.Ln, r=[fs], w=[G])
                k.op("vector", lambda e, G=G: e.tensor_tensor_scan(out=G[:, :], data0=cx.cmask[:, :N], data1=G[:, :],
                                                                    initial=0.0, op0=ALU.mult, op1=ALU.add),
                     r=[cx.cmask, G], w=[G])
                k.act(eG[:, :], G[:, :], AF.Exp, r=[G], w=[eG])
                k.act(eGn[:, :], G[:, :], AF.Exp, r=[G], w=[eGn], scale=-1.0)
                k.tt(self.Qt[hd][:, :], ps_q[:, :N], eG[:, :], ALU.mult, r=[ps_q, eG], w=[self.Qt[hd]])
                k.ts(fs[:, :], fs[:, :], -1.0, 1.0, ALU.mult, ALU.add, r=[fs], w=[fs])
                k.tt(self.Kt[hd][:, :], fs[:, :], eGn[:, :], ALU.mult, r=[fs, eGn], w=[self.Kt[hd]], eng="gpsimd")
                k.act(self.It[hd][:, :], ps_i[:, :N], AF.Copy, r=[ps_i], w=[self.It[hd]])
                k.act(self.sg[hd][:, :], ps_g[:, :N], AF.Silu, r=[ps_g], w=[self.sg[hd]])
                k.copy(self.dend[hd][:, :], v3(eG[:, :])[:, :, 63], r=[eG], w=[self.dend[hd]])
            for c in range(4):
                self.chunk(c)
            for hd in range(6):
                self.post(hd, n, t0, mixT)

    def chunk(self, c):
        k, cx = self.k, self.cx
        cs = slice(c * 64, (c + 1) * 64)
        for hd in range(6):
            Qt, Kt, It, H = self.Qt[hd], self.Kt[hd], self.It[hd], self.H[hd]
            pA = cx.bank()
            k.mm(pA[0:64, 0:64], Kt[:, cs], Qt[:, cs], r=[Kt, Qt], w=[pA])
            k.tt(self.AttT[hd][:, :], pA[0:64, 0:64], cx.m192[0:64, 128:192], ALU.mult, r=[pA, cx.m192],
                 w=[self.AttT[hd]])
            pT = cx.bank()
            k.op("tensor", lambda e, pT=pT, Kt=Kt: e.transpose(pT[0:64, 0:128], Kt[:, cs], cx.ident[:, :]),
                 r=[Kt, cx.ident], w=[pT], signal=False)
            k.op("tensor", lambda e, pT=pT, It=It: e.transpose(pT[0:64, 128:256], It[:, cs], cx.ident[:, :]),
                 r=[It, cx.ident], w=[pT], signal=True)
            k.act(self.TOK[hd][:, :], pT[0:64, 0:256], AF.Copy, r=[pT], w=[self.TOK[hd]])
            k.ts(self.Hd[hd][:, :], H[:, :], self.dend[hd][:, c:c + 1], None, ALU.mult, None,
                 r=[H, self.dend[hd]], w=[self.Hd[hd]], eng="gpsimd")
        for hd in range(6):
            Qt, H, TOK = self.Qt[hd], self.H[hd], self.TOK[hd]
            pO = cx.bank()
            k.mm(pO[:, 0:64], H[:, :], Qt[:, cs], r=[H, Qt], w=[pO], start=True, stop=False, signal=False)
            k.mm(pO[:, 0:64], TOK[:, 128:256], self.AttT[hd][:, :], r=[TOK, self.AttT[hd]], w=[pO],
                 start=False, stop=True)
            k.act(self.ot[hd][:, cs], pO[:, 0:64], AF.Copy, r=[pO], w=[self.ot[hd]], append=(c > 0))
            pH = cx.bank()
            k.mm(pH[:, 0:128], TOK[:, 0:128], TOK[:, 128:256], r=[TOK], w=[pH])
            k.stt(H[:, :], pH[:, 0:128], self.dend[hd][:, c:c + 1], self.Hd[hd][:, :], ALU.mult, ALU.add,
                  r=[pH, self.dend[hd], self.Hd[hd]], w=[H])

    def post(self, hd, n, t0, mixT):
        k, cx = self.k, self.cx
        N = TT_
        ot, tmp = self.ot[hd], self.tmp[hd % 2]
        k.act(tmp[:, :], ot[:, :], AF.Square, r=[ot], w=[tmp])
        pss = cx.bank()
        k.mm(pss[:, :N], cx.ones_f[:, :], tmp[:, :], r=[cx.ones_f, tmp], w=[pss])
        k.act(tmp[:, :], pss[:, :N], AF.Sqrt, r=[pss, cx.eps], w=[tmp], scale=1.0 / 128, bias=cx.eps[:, 0:1])
        k.op("vector", lambda e: e.reciprocal(out=tmp[:, :], in_=tmp[:, :]), r=[tmp], w=[tmp])
        k.tt(ot[:, :], ot[:, :], tmp[:, :], ALU.mult, r=[ot, tmp], w=[ot])
        yo = self.yo[hd % 2]
        k.stt(yo[:, :], ot[:, :], self.hgg[:, hd:hd + 1], self.sg[hd][:, :], ALU.mult, ALU.mult,
              r=[ot, self.hgg, self.sg[hd]], w=[yo])
        k.dma(mixT[256 + hd * 128:256 + (hd + 1) * 128, t0:t0 + N], yo[:, :], r=[yo], w=[mixT.b(("hg", hd, n))])


class S5Stage:
    def __init__(self, k, cx, o, S):
        self.k, self.cx, self.o, self.S = k, cx, o, S
        sb = k.sb
        f8 = lambda nm: sb([128, 8], F32, name=nm)
        self.ar, self.ai, self.dt, self.rho, self.th, self.thr = f8("ar"), f8("ai"), f8("dt"), f8("rho"), f8("th"), f8("thr")
        self.cs, self.sn, self.nsn, self.c1r, self.c1i, self.nc1i = f8("cs"), f8("sn"), f8("nsn"), f8("c1r"), f8("c1i"), f8("nc1i")
        self.t8 = [f8("t8") for _ in range(4)]
        self.halfpi = sb([128, 1], F32, name="halfpi")
        self.tabc = sb([128, S], F32, name="tabc")
        self.tabs = sb([128, S], F32, name="tabs")
        self.tt1 = sb([128, max(S // 2, 512)], F32, name="tt1")
        self.bre = sb([128, 16], F32, name="bre")
        self.bim = sb([128, 16], F32, name="bim")
        self.Bblk = [sb([128, 32], F32, name="Bblk") for _ in range(2)]
        self.Bl = [sb([32, 128], F32, name="Bl") for _ in range(2)]
        self.Cnat = [sb([32, 128], F32, name="Cnat") for _ in range(2)]
        self.Cl = [sb([128, 32], F32, name="Cl") for _ in range(2)]
        self.dcol = sb([32, 1], F32, name="dcol")
        self.cm = [sb([128, 1], F32, name="cm") for _ in range(4)]
        self.ut = [sb([32, 512], F32, name="ut") for _ in range(2)]
        f5 = lambda nm: sb([128, 512], F32, name=nm)
        self.ta, self.tb, self.br, self.bi, self.wr, self.wi, self.xr, self.xi = [f5(n_) for n_ in
                                                                                   ("ta", "tb", "br", "bi", "wr", "wi", "xr", "xi")]
        self.wr2, self.wi2 = f5("wr2"), f5("wi2")
        self.yv = [sb([32, 512], F32, name="yv") for _ in range(2)]
        self.y2 = sb([32, 512], F32, name="y2")

    def exp_taylor(self, dst, src, nsq, deg):
        import math
        k = self.k
        y, s_ = self.t8[2], self.t8[3]
        k.ts(y[:, :], src[:, :], 1.0 / (1 << nsq), None, ALU.mult, None, r=[src], w=[y])
        k.ts(s_[:, :], y[:, :], 1.0 / math.factorial(deg), None, ALU.mult, None, r=[y], w=[s_])
        for kk in range(deg - 1, 0, -1):
            k.stt(s_[:, :], s_[:, :], 1.0 / math.factorial(kk), y[:, :], ALU.add, ALU.mult, r=[s_, y], w=[s_])
        k.ts(dst[:, :], s_[:, :], 1.0, None, ALU.add, None, r=[s_], w=[dst])
        for _ in range(nsq):
            k.tt(dst[:, :], dst[:, :], dst[:, :], ALU.mult, r=[dst], w=[dst])

    def prep_scalars(self, P):
        k, o = self.k, self.o
        k.memset(self.halfpi[:, :], PI / 2, w=[self.halfpi])
        k.dma(self.ar[:, :], P["s5_a_re"][o].rearrange("(j g) p -> (g p) j", g=2), r=[], w=[self.ar],
              allow_slow_non_contiguous=True)
        k.dma(self.ai[:, :], P["s5_a_im"][o].rearrange("(j g) p -> (g p) j", g=2), r=[], w=[self.ai],
              allow_slow_non_contiguous=True)
        ld = P["s5_log_dt"][o].rearrange("(j g) -> g j", g=2)
        for g in range(2):
            k.dma(self.dt[g * 64:(g + 1) * 64, :], ld[g].partition_broadcast(64), r=[], w=[self.dt], append=(g > 0),
                  allow_slow_non_contiguous=True)
        self.exp_taylor(self.dt, self.dt, nsq=3, deg=13)
        k.ts(self.ar[:, :], self.ar[:, :], -1e-4, None, ALU.min, None, r=[self.ar], w=[self.ar])
        k.tt(self.rho[:, :], self.ar[:, :], self.dt[:, :], ALU.mult, r=[self.ar, self.dt], w=[self.rho])
        self.exp_taylor(self.rho, self.rho, nsq=2, deg=12)
        k.tt(self.th[:, :], self.ai[:, :], self.dt[:, :], ALU.mult, r=[self.ai, self.dt], w=[self.th])
        t0_, t1_ = self.t8[0], self.t8[1]
        k.copy(self.thr[:, :], self.th[:, :], r=[self.th], w=[self.thr])
        for m in (1, 3, 5, 7, 9, 11):
            k.ts(t0_[:, :], self.th[:, :], m * PI, 2 * PI, ALU.is_gt, ALU.mult, r=[self.th], w=[t0_])
            k.tt(self.thr[:, :], self.thr[:, :], t0_[:, :], ALU.subtract, r=[self.thr, t0_], w=[self.thr])
        k.act(self.sn[:, :], self.thr[:, :], AF.Sin, r=[self.thr], w=[self.sn])
        k.ts(t1_[:, :], self.thr[:, :], PI / 2, None, ALU.add, None, r=[self.thr], w=[t1_])
        k.ts(t0_[:, :], t1_[:, :], PI, 2 * PI, ALU.is_gt, ALU.mult, r=[t1_], w=[t0_])
        k.tt(t1_[:, :], t1_[:, :], t0_[:, :], ALU.subtract, r=[t1_, t0_], w=[t1_])
        k.act(self.cs[:, :], t1_[:, :], AF.Sin, r=[t1_], w=[self.cs])
        k.ts(self.nsn[:, :], self.sn[:, :], -1.0, None, ALU.mult, None, r=[self.sn], w=[self.nsn])
        nr, ni, den, t3 = self.t8[0], self.t8[1], self.t8[2], self.t8[3]
        k.tt(nr[:, :], self.rho[:, :], self.cs[:, :], ALU.mult, r=[self.rho, self.cs], w=[nr])
        k.ts(nr[:, :], nr[:, :], -1.0, None, ALU.add, None, r=[nr], w=[nr])
        k.tt(ni[:, :], self.rho[:, :], self.sn[:, :], ALU.mult, r=[self.rho, self.sn], w=[ni])
        k.tt(den[:, :], self.ar[:, :], self.ar[:, :], ALU.mult, r=[self.ar], w=[den])
        k.tt(t3[:, :], self.ai[:, :], self.ai[:, :], ALU.mult, r=[self.ai], w=[t3])
        k.tt(den[:, :], den[:, :], t3[:, :], ALU.add, r=[den, t3], w=[den])
        k.op("vector", lambda e: e.reciprocal(out=den[:, :], in_=den[:, :]), r=[den], w=[den])
        k.tt(self.c1r[:, :], nr[:, :], self.ar[:, :], ALU.mult, r=[nr, self.ar], w=[self.c1r])
        k.tt(t3[:, :], ni[:, :], self.ai[:, :], ALU.mult, r=[ni, self.ai], w=[t3])
        k.tt(self.c1r[:, :], self.c1r[:, :], t3[:, :], ALU.add, r=[self.c1r, t3], w=[self.c1r])
        k.tt(self.c1r[:, :], self.c1r[:, :], den[:, :], ALU.mult, r=[self.c1r, den], w=[self.c1r])
        k.tt(self.c1i[:, :], ni[:, :], self.ar[:, :], ALU.mult, r=[ni, self.ar], w=[self.c1i])
        k.tt(t3[:, :], nr[:, :], self.ai[:, :], ALU.mult, r=[nr, self.ai], w=[t3])
        k.tt(self.c1i[:, :], self.c1i[:, :], t3[:, :], ALU.subtract, r=[self.c1i, t3], w=[self.c1i])
        k.tt(self.c1i[:, :], self.c1i[:, :], den[:, :], ALU.mult, r=[self.c1i, den], w=[self.c1i])
        k.ts(self.nc1i[:, :], self.c1i[:, :], -1.0, None, ALU.mult, None, r=[self.c1i], w=[self.nc1i])

    def prep_tile(self, P, j):
        k, cx, o, S = self.k, self.cx, self.o, self.S
        col = lambda T_: T_[:, j:j + 1]
        k.dma(self.bre[:, :], P["s5_b_re"][o][2 * j:2 * j + 2].rearrange("g p c -> (g p) c"), r=[], w=[self.bre])
        k.dma(self.bim[:, :], P["s5_b_im"][o][2 * j:2 * j + 2].rearrange("g p c -> (g p) c"), r=[], w=[self.bim])
        for T_ in self.Bblk:
            k.memset(T_[:, :], 0.0, w=[T_])
        for g in range(2):
            gs = slice(g * 64, (g + 1) * 64)
            gc = slice(g * 16, (g + 1) * 16)
            k.ts(self.Bblk[0][gs, gc], self.bre[gs, :], self.c1r[gs, j:j + 1], None, ALU.mult, None,
                 r=[self.bre, self.c1r], w=[self.Bblk[0]])
            k.stt(self.Bblk[0][gs, gc], self.bim[gs, :], self.nc1i[gs, j:j + 1], self.Bblk[0][gs, gc], ALU.mult, ALU.add,
                  r=[self.bim, self.nc1i, self.Bblk[0]], w=[self.Bblk[0]])
            k.ts(self.Bblk[1][gs, gc], self.bim[gs, :], self.c1r[gs, j:j + 1], None, ALU.mult, None,
                 r=[self.bim, self.c1r], w=[self.Bblk[1]])
            k.stt(self.Bblk[1][gs, gc], self.bre[gs, :], self.c1i[gs, j:j + 1], self.Bblk[1][gs, gc], ALU.mult, ALU.add,
                  r=[self.bre, self.c1i, self.Bblk[1]], w=[self.Bblk[1]])
        for i in range(2):
            pt = cx.bank()
            k.op("tensor", lambda e, pt=pt, i=i: e.transpose(pt[0:32, 0:128], self.Bblk[i][:, :], cx.ident[:, :]),
                 r=[self.Bblk[i], cx.ident], w=[pt])
            k.act(self.Bl[i][:, :], pt[0:32, 0:128], AF.Copy, r=[pt], w=[self.Bl[i]])
        for i, nm in enumerate(("s5_c_re", "s5_c_im")):
            k.memset(self.Cnat[i][:, :], 0.0, w=[self.Cnat[i]])
            for g in range(2):
                k.dma(self.Cnat[i][g * 16:(g + 1) * 16, g * 64:(g + 1) * 64], P[nm][o][2 * j + g], r=[],
                      w=[self.Cnat[i]], append=(g > 0))
            pt = cx.bank()
            k.op("tensor", lambda e, pt=pt, i=i: e.transpose(pt[:, 0:32], self.Cnat[i][:, :], cx.ident[0:32, 0:32]),
                 r=[self.Cnat[i], cx.ident], w=[pt])
            k.act(self.Cl[i][:, :], pt[:, 0:32], AF.Copy, r=[pt], w=[self.Cl[i]], scale=(1.0 if i == 0 else -1.0))
        k.dma(self.dcol[:, 0:1], P["s5_d"][o][32 * j:32 * j + 32].rearrange("(p o) -> p o", o=1), r=[], w=[self.dcol])
        tc_, ts_ = self.tabc, self.tabs
        k.memset(tc_[:, 0:1], 1.0, w=[tc_])
        k.memset(ts_[:, 0:1], 0.0, w=[ts_])
        cm, sm, nsm = self.cm[0], self.cm[1], self.cm[2]
        k.copy(cm[:, :], col(self.cs), r=[self.cs], w=[cm])
        k.copy(sm[:, :], col(self.sn), r=[self.sn], w=[sm])
        L = 1
        while L < S:
            k.ts(nsm[:, :], sm[:, :], -1.0, None, ALU.mult, None, r=[sm], w=[nsm])
            t1 = self.tt1
            k.ts(t1[:, 0:L], tc_[:, 0:L], cm[:, 0:1], None, ALU.mult, None, r=[tc_, cm], w=[t1])
            k.stt(tc_[:, L:2 * L], ts_[:, 0:L], nsm[:, 0:1], t1[:, 0:L], ALU.mult, ALU.add, r=[ts_, nsm, t1], w=[tc_])
            k.ts(t1[:, 0:L], tc_[:, 0:L], sm[:, 0:1], None, ALU.mult, None, r=[tc_, sm], w=[t1])
            k.stt(ts_[:, L:2 * L], ts_[:, 0:L], cm[:, 0:1], t1[:, 0:L], ALU.mult, ALU.add, r=[ts_, cm, t1], w=[ts_])
            L *= 2
            if L < S:
                e1 = 2 * L // 2 - 1
                t8a = self.cm[3]
                k.ts(t8a[:, :], tc_[:, e1:e1 + 1], col(self.cs), None, ALU.mult, None, r=[tc_, self.cs], w=[t8a])
                k.stt(cm[:, :], ts_[:, e1:e1 + 1], col(self.nsn), t8a[:, :], ALU.mult, ALU.add,
                      r=[ts_, self.nsn, t8a], w=[cm])
                k.ts(t8a[:, :], tc_[:, e1:e1 + 1], col(self.sn), None, ALU.mult, None, r=[tc_, self.sn], w=[t8a])
                k.stt(sm[:, :], ts_[:, e1:e1 + 1], col(self.cs), t8a[:, :], ALU.mult, ALU.add,
                      r=[ts_, self.cs, t8a], w=[sm])

    def run(self, P, TC, s5u, s5y, dbg=None):
        k, cx, S = self.k, self.cx, self.S
        self.dbg = dbg
        self.prep_scalars(P)
        CH = 512
        it = 0
        for j in range(8):
            self.prep_tile(P, j)
            rho_bc = self.rho[:, j:j + 1].to_broadcast([128, CH])
            for b in range(TC // S):
                for tcn in range(S // CH):
                    c0 = b * S + tcn * CH
                    ls = slice(tcn * CH, (tcn + 1) * CH)
                    ut = self.ut[it % 2]
                    yv = self.yv[it % 2]
                    it += 1
                    k.dma(ut[:, :], s5u[32 * j:32 * j + 32, c0:c0 + CH], r=[], w=[ut])
                    pr, pi = cx.bank(), cx.bank()
                    k.mm(pr[:, :CH], self.Bl[0][:, :], ut[:, :], r=[self.Bl[0], ut], w=[pr])
                    k.mm(pi[:, :CH], self.Bl[1][:, :], ut[:, :], r=[self.Bl[1], ut], w=[pi])
                    ta, tb, br, bi = self.ta, self.tb, self.br, self.bi
                    k.tt(ta[:, :], pr[:, :CH], self.tabc[:, ls], ALU.mult, r=[pr, self.tabc], w=[ta])
                    k.tt(tb[:, :], pi[:, :CH], self.tabs[:, ls], ALU.mult, r=[pi, self.tabs], w=[tb], eng="vector")
                    k.tt(br[:, :], ta[:, :], tb[:, :], ALU.add, r=[ta, tb], w=[br], eng="gpsimd")
                    k.tt(ta[:, :], pi[:, :CH], self.tabc[:, ls], ALU.mult, r=[pi, self.tabc], w=[ta])
                    k.tt(tb[:, :], pr[:, :CH], self.tabs[:, ls], ALU.mult, r=[pr, self.tabs], w=[tb])
                    k.tt(bi[:, :], ta[:, :], tb[:, :], ALU.subtract, r=[ta, tb], w=[bi], eng="gpsimd")
                    wr, wi = (self.wr, self.wi) if (tcn % 2 == 0) else (self.wr2, self.wi2)
                    pwr, pwi = (self.wr2, self.wi2) if (tcn % 2 == 0) else (self.wr, self.wi)
                    for (w_, pw_, b_) in ((wr, pwr, br), (wi, pwi, bi)):
                        if tcn > 0:
                            k.stt(b_[:, 0:1], pw_[:, CH - 1:CH], self.rho[:, j:j + 1], b_[:, 0:1], ALU.mult, ALU.add,
                                  r=[pw_, self.rho, b_], w=[b_])
                        k.op("vector", lambda e, w_=w_, b_=b_: e.tensor_tensor_scan(
                            out=w_[:, :], data0=rho_bc, data1=b_[:, :], initial=0.0, op0=ALU.mult, op1=ALU.add),
                             r=[self.rho, b_], w=[w_])
                    xr, xi = self.xr, self.xi
                    k.tt(ta[:, :], wr[:, :], self.tabc[:, ls], ALU.mult, r=[wr, self.tabc], w=[ta])
                    k.tt(tb[:, :], wi[:, :], self.tabs[:, ls], ALU.mult, r=[wi, self.tabs], w=[tb], eng="gpsimd")
                    k.tt(xr[:, :], ta[:, :], tb[:, :], ALU.subtract, r=[ta, tb], w=[xr])
                    k.tt(ta[:, :], wr[:, :], self.tabs[:, ls], ALU.mult, r=[wr, self.tabs], w=[ta])
                    k.tt(tb[:, :], wi[:, :], self.tabc[:, ls], ALU.mult, r=[wi, self.tabc], w=[tb], eng="gpsimd")
                    k.tt(xi[:, :], ta[:, :], tb[:, :], ALU.add, r=[ta, tb], w=[xi])
                    py = cx.bank()
                    k.mm(py[0:32, :CH], self.Cl[0][:, :], xr[:, :], r=[self.Cl[0], xr], w=[py], start=True, stop=False,
                         signal=False)
                    k.mm(py[0:32, :CH], self.Cl[1][:, :], xi[:, :], r=[self.Cl[1], xi], w=[py], start=False, stop=True)
                    k.stt(yv[:, :], ut[:, :], self.dcol[:, 0:1], py[0:32, :CH], ALU.mult, ALU.add,
                          r=[ut, self.dcol, py], w=[yv])
                    y2 = self.y2
                    k.act(y2[:, :], yv[:, :], AF.Square, r=[yv], w=[y2])
                    k.ts(y2[:, :], y2[:, :], 0.044715, 1.0, ALU.mult, ALU.add, r=[y2], w=[y2])
                    k.tt(y2[:, :], y2[:, :], yv[:, :], ALU.mult, r=[y2, yv], w=[y2])
                    k.act(y2[:, :], y2[:, :], AF.Sigmoid, r=[y2], w=[y2], scale=1.5957691216057308)
                    k.tt(yv[:, :], yv[:, :], y2[:, :], ALU.mult, r=[yv, y2], w=[yv])
                    k.dma(s5y[32 * j:32 * j + 32, c0:c0 + CH], yv[:, :], r=[yv], w=[s5y.b((j, b, tcn))])
        if self.dbg is not None:
            for nm in ("tabc", "tabs", "rho", "thr", "th", "cs", "sn", "c1r", "c1i", "dt", "br", "bi", "wr", "wi", "xr", "xi"):
                T_ = getattr(self, nm)
                k.dma(self.dbg[nm][:, :], T_[:, :], r=[T_], w=[self.dbg[nm].b(0)])
            for i in range(2):
                k.dma(self.dbg["Bl%d" % i][:, :], self.Bl[i][:, :], r=[self.Bl[i]], w=[self.dbg["Bl%d" % i].b(0)])
                k.dma(self.dbg["Cl%d" % i][:, :], self.Cl[i][:, :], r=[self.Cl[i]], w=[self.dbg["Cl%d" % i].b(0)])


class OddOut:
    def __init__(self, k, cx):
        self.k, self.cx = k, cx
        sb = k.sb
        self.Wo = [sb([128, D], BF16, name="Wo") for _ in range(KT)]
        self.Wg = [sb([128, 256], BF16, name="Wg") for _ in range(2)]
        self.bg = sb([128, 2], F32, name="bg")
        self.xt = [sb([128, KT, NT], F32, name="o_xt") for _ in range(2)]
        self.mt = [sb([128, KT, NT], BF16, name="o_mt") for _ in range(2)]
        self.yf = [sb([128, 2, NT], F32, name="o_yf") for _ in range(2)]
        self.yb = sb([128, 2, NT], BF16, name="o_yb")
        self.sg = sb([128, NT], F32, name="o_sg")

    def load_weights(self, P, o):
        k = self.k
        for kt in range(KT):
            k.dma(self.Wo[kt][:, :], P["od_w_out"][o][kt * 128:(kt + 1) * 128, :], r=[], w=[self.Wo[kt]], q="gpsimd")
        for kt in range(2):
            k.dma(self.Wg[kt][:, :], P["s5_w_glu"][o][kt * 128:(kt + 1) * 128, :], r=[], w=[self.Wg[kt]], q="gpsimd")
        load_col(k, self.bg, P["s5_b_glu"][o], 256, r=[])

    def run(self, src, dst, mixT, s5y, TC):
        k, cx = self.k, self.cx
        for n in range(TC // NT):
            t0 = n * NT
            xt, mt, yf = self.xt[n % 2], self.mt[n % 2], self.yf[n % 2]
            k.dma(xt[:, :, :], src[:, t0:t0 + NT].rearrange("(kt p) n -> p kt n", p=128), r=src.bs(t0, t0 + NT), w=[xt])
            k.dma(mt[:, 2:8, :], mixT[256:1024, t0:t0 + NT].rearrange("(kt p) n -> p kt n", p=128), r=[], w=[mt])
            k.dma(yf[:, :, :], s5y[:, t0:t0 + NT].rearrange("(kt p) n -> p kt n", p=128), r=[], w=[yf])
            k.copy(self.yb[:, :, :], yf[:, :, :], r=[yf], w=[self.yb])
            for m in range(2):
                pg = cx.bank()
                for kt in range(2):
                    k.mm(pg[:, :NT], self.Wg[kt][:, m * 128:(m + 1) * 128], self.yb[:, kt, :], r=[self.Wg[kt], self.yb],
                         w=[pg], start=(kt == 0), stop=(kt == 1), signal=(kt == 1))
                k.act(self.sg[:, :], pg[:, :NT], AF.Sigmoid, r=[pg, self.bg], w=[self.sg], bias=self.bg[:, m:m + 1])
                k.tt(mt[:, m, :], yf[:, m, :], self.sg[:, :], ALU.mult, r=[yf, self.sg], w=[mt], append=True)
            for m in range(KT):
                po = cx.bank()
                for kt in range(KT):
                    k.mm(po[:, :NT], self.Wo[kt][:, m * 128:(m + 1) * 128], mt[:, kt, :], r=[self.Wo[kt], mt], w=[po],
                         start=(kt == 0), stop=(kt == KT - 1), signal=(kt == KT - 1))
                k.tt(xt[:, m, :], xt[:, m, :], po[:, :NT], ALU.add, r=[xt, po], w=[xt])
            k.dma(dst[:, t0:t0 + NT].rearrange("(kt p) n -> p kt n", p=128), xt[:, :, :], r=[xt], w=dst.bs(t0, t0 + NT))


def odd_mixer_layer(k, cx, P, o, src, dst, S, TC, scr, dbg=None):
    import os
    mask = os.environ.get("ODD_STAGES", "123")
    if "1" in mask:
        with k.scope():
            op_ = OddProj(k, cx, o)
            op_.load_weights(P)
            op_.run(src, S, TC, scr["s5u"], scr["mixT"])
    if "2" in mask:
        with k.scope():
            s5 = S5Stage(k, cx, o, S)
            s5.run(P, TC, scr["s5u"], scr["s5y"], dbg=dbg)
    if "3" in mask:
        with k.scope():
            oo = OddOut(k, cx)
            oo.load_weights(P, o)
            oo.run(src, dst, scr["mixT"], scr["s5y"], TC)


B_FULL, S_FULL, N_CORES = 16, 4096, 8
DEPTH = 4
W_NAMES = ["ffn1_norm", "ffn1_w13", "ffn1_w2", "mix_norm", "ffn2_norm", "ffn2_w13", "ffn2_w2", "ev_w_in", "ev_w_out",
           "rw_mu", "rw_w0", "rw_w2", "rw_a0", "rw_a2", "rw_g2", "rw_k_k", "rw_k_a", "rw_r_k", "rw_ln_w", "rw_ln_b",
           "rw_v0", "rw_v1", "rw_v2", "sb_q_gain", "sb_k_gain", "od_w_in", "od_w_out", "s5_a_re", "s5_a_im", "s5_b_re",
           "s5_b_im", "s5_c_re", "s5_c_im", "s5_d", "s5_log_dt", "s5_w_glu", "s5_b_glu", "hg_lb", "hg_gain"]


def ffn_layer(k, cx, w13, w2, gain, src, dst, TC):
    with k.scope():
        f = FFN(k, cx)
        f.load_weights(w13, w2, gain)
        f.run(src, dst, TC)


def build_program(shapes, S, BC, depth=DEPTH):
    TC = S * BC
    k = K()
    cx = Ctx(k)
    build_consts(k, cx)
    P = {nm: k.nc.dram_tensor(nm, list(shapes[nm]), F32, kind="ExternalInput").ap() for nm in W_NAMES}
    x_in = DR(k, "xT", [D, TC], F32, kind="ExternalInput")
    y_out = DR(k, "yT", [D, TC], F32, kind="ExternalOutput")
    pp = [DR(k, "xa", [D, TC], F32), DR(k, "xb", [D, TC], F32)]
    scr = dict(vfirst=DR(k, "vfirst", [512, TC], F32), sbq=DR(k, "sbq", [512, TC], BF16),
               sbk=DR(k, "sbk", [512, TC], BF16), sbv=DR(k, "sbv", [TC, 512], BF16),
               mixT=DR(k, "mixT", [1024, TC], BF16), s5u=DR(k, "s5u", [256, TC], F32),
               s5y=DR(k, "s5y", [256, TC], F32))
    n_stage = 3 * depth
    bufs = [x_in] + [pp[i % 2] for i in range(n_stage - 1)] + [y_out]
    si = 0
    for layer in range(depth):
        ffn_layer(k, cx, P["ffn1_w13"][layer], P["ffn1_w2"][layer], P["ffn1_norm"][layer], bufs[si], bufs[si + 1], TC)
        si += 1
        if layer % 2 == 0:
            even_mixer_layer(k, cx, P, layer // 2, bufs[si], bufs[si + 1], S, TC, scr)
        else:
            odd_mixer_layer(k, cx, P, layer // 2, bufs[si], bufs[si + 1], S, TC, scr)
        si += 1
        ffn_layer(k, cx, P["ffn2_w13"][layer], P["ffn2_w2"][layer], P["ffn2_norm"][layer], bufs[si], bufs[si + 1], TC)
        si += 1
    k.finish(list(y_out.bufs.values()))
    return k


def kernel(**inputs):
    x = np.asarray(inputs["x"], dtype=np.float32)
    B, S, Dm = x.shape
    BC = B // N_CORES
    shapes = {nm: tuple(np.asarray(inputs[nm]).shape) for nm in W_NAMES}
    k = build_program(shapes, S, BC)
    weights = {nm: np.ascontiguousarray(np.asarray(inputs[nm], dtype=np.float32)) for nm in W_NAMES}
    in_maps = []
    for c in range(N_CORES):
        xT = np.ascontiguousarray(x[c * BC:(c + 1) * BC].reshape(BC * S, Dm).T)
        m = dict(weights)
        m["xT"] = xT
        in_maps.append(m)
    res = run_bass_kernel_spmd(k.nc, in_maps, core_ids=list(range(N_CORES)))
    out = np.empty((B, S, Dm), dtype=np.float32)
    for c in range(N_CORES):
        yT = np.asarray(res.results[c]["yT"], dtype=np.float32)
        out[c * BC:(c + 1) * BC] = yT.T.reshape(BC, S, Dm)
    return out
```

```python
from concourse.bass_utils import run_bass_kernel_spmd

import contextlib
import numpy as np
import concourse.bass as bass
import concourse.mybir as mybir

F32 = mybir.dt.float32
BF16 = mybir.dt.bfloat16
AF = mybir.ActivationFunctionType
ALU = mybir.AluOpType
AX = mybir.AxisListType


class Buf:
    __slots__ = ("name", "w", "r")

    def __init__(self, name=""):
        self.name = name
        self.w = []
        self.r = {}


class Tn(Buf):
    __slots__ = ("t",)

    def __init__(self, t, name=""):
        super().__init__(name)
        self.t = t

    def __getitem__(self, idx):
        return self.t[idx]


class Eng:
    def __init__(self, name, eng, sem):
        self.name = name
        self.eng = eng
        self.sem = sem
        self.count = 0
        self.waited = {}


class K:
    def __init__(self, n_dsem=24):
        self.nc = bass.Bass("TRN2", target_bir_lowering=False)
        self.es = contextlib.ExitStack()
        nc = self.nc
        self.sems = {}
        self.engs = {}
        for nm, e in (("tensor", nc.tensor), ("vector", nc.vector), ("scalar", nc.scalar),
                      ("gpsimd", nc.gpsimd), ("sync", nc.sync)):
            s = self.es.enter_context(nc.semaphore("s_" + nm))
            self.sems["s_" + nm] = s
            self.engs[nm] = Eng(nm, e, "s_" + nm)
        self.dsems = []
        for i in range(n_dsem):
            key = "d%d" % i
            self.sems[key] = self.es.enter_context(nc.semaphore(key))
            self.dsems.append([key, 0])
        self.dnext = 0
        self.n_inst = 0
        self.uid = 0
        self.scopes = [self.es]

    def sb(self, shape, dtype=F32, name=None):
        self.uid += 1
        name = (name or "sb") + "_%d" % self.uid
        t = self.scopes[-1].enter_context(self.nc.sbuf_tensor(name, list(shape), dtype))
        return Tn(t, name)

    def ps(self, shape, dtype=F32, name=None):
        self.uid += 1
        name = (name or "ps") + "_%d" % self.uid
        t = self.scopes[-1].enter_context(self.nc.psum_tensor(name, list(shape), dtype))
        return Tn(t, name)

    @contextlib.contextmanager
    def scope(self):
        st = contextlib.ExitStack()
        self.scopes.append(st)
        try:
            yield
        finally:
            self.barrier()
            self.scopes.pop()
            st.close()

    def barrier(self):
        for E in self.engs.values():
            for F in self.engs.values():
                if F is not E and F.count > 0:
                    self._wait(E, (F.sem, F.count))
            for d in self.dsems:
                if d[1] > 0:
                    self._wait(E, (d[0], d[1]))

    def dram(self, name, shape, dtype=F32, kind="Internal"):
        t = self.nc.dram_tensor(name, list(shape), dtype, kind=kind)
        return Tn(t.ap(), name)

    def _wait(self, E, tok):
        if tok is None:
            return
        key, val = tok
        if E.waited.get(key, 0) >= val:
            return
        if key == E.sem and E.name == "tensor":
            return
        E.eng.wait_ge(self.sems[key], val)
        E.waited[key] = val
        self.n_inst += 1

    def _deps(self, E, r, w, append=False):
        for b in r:
            for t in b.w:
                self._wait(E, t)
        for b in w:
            if not append:
                for t in b.w:
                    self._wait(E, t)
            for key, val in list(b.r.items()):
                self._wait(E, (key, val))

    def _mark(self, tok, r, w, append=False):
        key, val = tok
        for b in r:
            if b.r.get(key, 0) < val:
                b.r[key] = val
        for b in w:
            if append:
                b.w = b.w + [tok]
            else:
                b.w = [tok]
            b.r = {}

    def op(self, eng, fn, r=(), w=(), signal=True, append=False):
        E = self.engs[eng]
        self._deps(E, r, w, append)
        inst = fn(E.eng)
        self.n_inst += 1
        tok = (E.sem, E.count + 1)
        if signal:
            inst.then_inc(self.sems[E.sem], 1)
            E.count += 1
        self._mark(tok, r, w, append)
        return inst

    def dma(self, out, in_, r=(), w=(), q="sync", append=False, **kw):
        E = self.engs[q]
        d = self.dsems[self.dnext]
        self.dnext = (self.dnext + 1) % len(self.dsems)
        if d[1] > 0:
            self._wait(E, (d[0], d[1]))
        self._deps(E, r, w, append)
        inst = E.eng.dma_start(out=out, in_=in_, **kw)
        d[1] += 16
        inst.then_inc(self.sems[d[0]], 16)
        self.n_inst += 1
        tok = (d[0], d[1])
        self._mark(tok, r, w, append)
        return inst

    def finish(self, bufs):
        E = self.engs["sync"]
        for b in bufs:
            for t in b.w:
                self._wait(E, t)
        for d in self.dsems:
            if d[1] > 0:
                self._wait(E, (d[0], d[1]))

    def mm(self, out, lhsT, rhs, r, w, start=True, stop=True, signal=True, **kw):
        return self.op("tensor", lambda e: e.matmul(out, lhsT, rhs, start=start, stop=stop, **kw),
                       r=r, w=w, signal=signal)

    def act(self, out, in_, func, r, w, eng="scalar", append=False, **kw):
        return self.op(eng, lambda e: e.activation(out=out, in_=in_, func=func, **kw), r=r, w=w, append=append)

    def tt(self, out, in0, in1, op, r, w, eng="vector", append=False):
        return self.op(eng, lambda e: e.tensor_tensor(out=out, in0=in0, in1=in1, op=op), r=r, w=w, append=append)

    def ts(self, out, in0, s1, s2, op0, op1, r, w, eng="vector", append=False):
        if op1 is None:
            return self.op(eng, lambda e: e.tensor_scalar(out=out, in0=in0, scalar1=s1, scalar2=None,
                                                           op0=op0), r=r, w=w, append=append)
        return self.op(eng, lambda e: e.tensor_scalar(out=out, in0=in0, scalar1=s1, scalar2=s2,
                                                       op0=op0, op1=op1), r=r, w=w, append=append)

    def stt(self, out, in0, scalar, in1, op0, op1, r, w, append=False):
        return self.op("vector", lambda e: e.scalar_tensor_tensor(out=out, in0=in0, scalar=scalar,
                                                                   in1=in1, op0=op0, op1=op1), r=r, w=w, append=append)

    def copy(self, out, in_, r, w, eng="vector", append=False):
        return self.op(eng, lambda e: e.tensor_copy(out=out, in_=in_), r=r, w=w, append=append)

    def memset(self, ap, val, w, eng="vector"):
        return self.op(eng, lambda e: e.memset(ap, val), r=(), w=w)


D = 1024
DFF = 2816
KT = D // 128
JT = DFF // 128
NT = 256
EPS = 1e-6


class DR:
    def __init__(self, k, name, shape, dtype=F32, kind="Internal"):
        self.t = k.nc.dram_tensor(name, list(shape), dtype, kind=kind).ap()
        self.bufs = {}
        self.name = name

    def b(self, key):
        if key not in self.bufs:
            self.bufs[key] = Buf("%s_%s" % (self.name, key))
        return self.bufs[key]

    def bs(self, lo, hi, g=256):
        return [self.b(i) for i in range(lo // g, (hi + g - 1) // g)]

    def __getitem__(self, idx):
        return self.t[idx]


class BankRef:
    def __init__(self, cx, tn, gen):
        self.cx, self.tn, self.gen = cx, tn, gen

    def _chk(self):
        assert self.cx.gen[self.tn.name] == self.gen, "stale PSUM bank use: " + self.tn.name

    def __getitem__(self, idx):
        self._chk()
        return self.tn.t[idx]

    @property
    def w(self):
        self._chk()
        return self.tn.w

    @w.setter
    def w(self, v):
        self.tn.w = v

    @property
    def r(self):
        self._chk()
        return self.tn.r

    @r.setter
    def r(self, v):
        self.tn.r = v


class Ctx:
    def __init__(self, k):
        self.k = k
        self.gen = {}
        self.banks = [k.ps([128, 512], F32, name="bank%d" % i) for i in range(8)]
        self.bi = 0
        self.nrot = 8
        self.ones_bf = k.sb([128, 128], BF16, name="ones_bf")
        k.memset(self.ones_bf[:], 1.0, w=[self.ones_bf])
        self.ones_f = k.sb([128, 128], F32, name="ones_f")
        k.memset(self.ones_f[:], 1.0, w=[self.ones_f])
        self.eps = k.sb([128, 1], F32, name="eps_c")
        k.memset(self.eps[:], EPS, w=[self.eps])
        self.colstage = k.sb([8, 128], F32, name="colstage")
        k.cx = self
        self.ident = k.sb([128, 128], F32, name="ident")
        k.op("gpsimd", lambda e: e.affine_select(out=self.ident[:], in_=self.ones_f[:], pattern=[[-1, 128]],
                                                 compare_op=ALU.is_equal, fill=0.0, base=0, channel_multiplier=1),
             r=[self.ones_f], w=[self.ident])

    def fixed(self, i):
        b = self.banks[i]
        self.gen[b.name] = self.gen.get(b.name, 0) + 1
        return BankRef(self, b, self.gen[b.name])

    def bank(self):
        self.bi = self.bi % self.nrot
        b = self.banks[self.bi]
        self.bi = (self.bi + 1) % self.nrot
        self.gen[b.name] = self.gen.get(b.name, 0) + 1
        return BankRef(self, b, self.gen[b.name])


def load_col(k, dst, vec_ap, n, r=(), eng_q="sync"):
    cx = k.cx
    c = n // 128
    st = cx.colstage
    k.dma(st[0:c, :], vec_ap.rearrange("(c p) -> c p", p=128), r=[], w=[st], q=eng_q)
    ps = cx.bank()
    k.op("tensor", lambda e: e.transpose(ps[:, 0:c], st[0:c, :], cx.ident[0:c, 0:c]), r=[st, cx.ident], w=[ps])
    k.copy(dst[:, 0:c], ps[:, 0:c], r=[ps], w=[dst])


def rmsnorm_tile(k, cx, xt, gain, hT, sq, rstd, ntok):
    ps = cx.bank()
    k.act(sq[:, :, :ntok], xt[:, :, :ntok], AF.Square, r=[xt], w=[sq])
    for kt in range(KT):
        k.mm(ps[:, :ntok], cx.ones_bf[:], sq[:, kt, :ntok], r=[cx.ones_bf, sq], w=[ps],
             start=(kt == 0), stop=(kt == KT - 1), signal=(kt == KT - 1))
    k.act(rstd[:, :ntok], ps[:, :ntok], AF.Sqrt, r=[ps, cx.eps], w=[rstd], scale=1.0 / D, bias=cx.eps[:, 0:1])
    k.op("vector", lambda e: e.reciprocal(out=rstd[:, :ntok], in_=rstd[:, :ntok]), r=[rstd], w=[rstd])
    k.tt(hT[:, :, :ntok], xt[:, :, :ntok], rstd[:, :ntok].unsqueeze(1).to_broadcast([128, KT, ntok]), ALU.mult,
         r=[xt, rstd], w=[hT])


class FFN:
    def __init__(self, k, cx):
        self.k = k
        self.cx = cx
        self.w13b = [k.sb([128, 2 * DFF], BF16, name="w13b%d" % i) for i in range(KT)]
        self.w2b = [k.sb([128, D], BF16, name="w2b%d" % j) for j in range(JT)]
        self.xt = [k.sb([128, KT, NT], F32, name="ffn_xt%d" % i) for i in range(2)]
        self.sq = k.sb([128, KT, NT], BF16, name="ffn_sq")
        self.hT = k.sb([128, KT, NT], BF16, name="ffn_hT")
        self.actT = k.sb([128, JT, NT], BF16, name="ffn_act")
        self.rstd = k.sb([128, NT], F32, name="ffn_rstd")
        self.sg = [k.sb([128, NT], F32, name="ffn_sg%d" % i) for i in range(2)]
        self.gain = k.sb([128, KT], F32, name="ffn_gain")

    def load_weights(self, w13_ap, w2_ap, gain_ap):
        k = self.k
        CH = 1408
        for kt in range(KT):
            for c in range(2 * DFF // CH):
                k.dma(self.w13b[kt][:, c * CH:(c + 1) * CH], w13_ap[kt * 128:(kt + 1) * 128, c * CH:(c + 1) * CH],
                      r=[], w=[self.w13b[kt]], q="gpsimd", append=(c > 0))
        for j in range(JT):
            k.dma(self.w2b[j][:, :], w2_ap[j * 128:(j + 1) * 128, :], r=[], w=[self.w2b[j]], q="gpsimd")
        load_col(k, self.gain, gain_ap, D, r=[])
        for kt in range(KT):
            k.ts(self.w13b[kt][:, :], self.w13b[kt][:, :], self.gain[:, kt:kt + 1], 1.0, ALU.mult, ALU.mult,
                 r=[self.gain, self.w13b[kt]], w=[self.w13b[kt]], eng="gpsimd")

    def run(self, src, dst, TC):
        k, cx = self.k, self.cx
        ntiles = TC // NT
        for n in range(ntiles):
            t0 = n * NT
            xt = self.xt[n % 2]
            k.dma(xt[:, :, :], src[:, t0:t0 + NT].rearrange("(kt p) n -> p kt n", p=128),
                  r=src.bs(t0, t0 + NT), w=[xt])
            rmsnorm_tile(k, cx, xt, self.gain, self.hT, self.sq, self.rstd, NT)
            for j in range(JT):
                pg = cx.bank()
                pu = cx.bank()
                for kt in range(KT):
                    k.mm(pg[:, :NT], self.w13b[kt][:, j * 128:(j + 1) * 128], self.hT[:, kt, :],
                         r=[self.w13b[kt], self.hT], w=[pg], start=(kt == 0), stop=(kt == KT - 1),
                         signal=(kt == KT - 1))
                for kt in range(KT):
                    k.mm(pu[:, :NT], self.w13b[kt][:, DFF + j * 128:DFF + (j + 1) * 128], self.hT[:, kt, :],
                         r=[self.w13b[kt], self.hT], w=[pu], start=(kt == 0), stop=(kt == KT - 1),
                         signal=(kt == KT - 1))
                sg = self.sg[j % 2]
                k.act(sg[:, :], pg[:, :NT], AF.Silu, r=[pg], w=[sg])
                k.tt(self.actT[:, j, :], sg[:, :], pu[:, :NT], ALU.mult, r=[sg, pu], w=[self.actT], append=(j > 0))
            for m in range(KT):
                po = cx.bank()
                for j in range(JT):
                    k.mm(po[:, :NT], self.w2b[j][:, m * 128:(m + 1) * 128], self.actT[:, j, :],
                         r=[self.w2b[j], self.actT], w=[po], start=(j == 0), stop=(j == JT - 1),
                         signal=(j == JT - 1))
                k.stt(xt[:, m, :], po[:, :NT], 0.5, xt[:, m, :], ALU.mult, ALU.add, r=[po, xt], w=[xt])
            k.dma(dst[:, t0:t0 + NT].rearrange("(kt p) n -> p kt n", p=128), xt[:, :, :],
                  r=[xt], w=dst.bs(t0, t0 + NT))


def build_consts(k, cx):
    c = cx
    c.bones = k.sb([128, 128], F32, name="bones")
    k.memset(c.bones[:], 0.0, w=[c.bones])
    k.memset(c.bones[0:64, 0:64], 1.0, w=[c.bones])
    k.memset(c.bones[64:128, 64:128], 1.0, w=[c.bones])
    c.m192 = k.sb([128, 192], F32, name="m192")
    k.op("gpsimd", lambda e: e.affine_select(out=c.m192[:, 0:128], in_=c.ones_f[:], pattern=[[1, 128]],
                                             compare_op=ALU.is_gt, fill=0.0, base=0, channel_multiplier=-1),
         r=[c.ones_f], w=[c.m192])
    k.memset(c.m192[0:64, 64:128], 0.0, w=[c.m192], eng="gpsimd")
    for h in range(2):
        k.op("gpsimd", lambda e, h=h: e.affine_select(out=c.m192[h * 64:(h + 1) * 64, 128:192],
                                                      in_=c.ones_f[h * 64:(h + 1) * 64, 0:64], pattern=[[1, 64]],
                                                      compare_op=ALU.is_ge, fill=0.0, base=0, channel_multiplier=-1),
             r=[c.ones_f], w=[c.m192])
    c.msl = k.sb([128, 128], F32, name="msl")
    k.op("gpsimd", lambda e: e.affine_select(out=c.msl[:], in_=c.ones_f[:], pattern=[[-1, 128]],
                                             compare_op=ALU.is_gt, fill=0.0, base=0, channel_multiplier=1),
         r=[c.ones_f], w=[c.msl])
    k.memset(c.msl[64:128, 0:64], 0.0, w=[c.msl], eng="gpsimd")
    c.cmask = k.sb([128, 512], F32, name="cmask")
    k.memset(c.cmask[:], 1.0, w=[c.cmask])
    k.memset(c.cmask[:, :].rearrange("p (c t) -> p c t", t=64)[:, :, 0:1], 0.0, w=[c.cmask])
    c.one_c = k.sb([128, 1], F32, name="one_c")
    k.memset(c.one_c[:], 1.0, w=[c.one_c])
    c.tiny_c = k.sb([128, 1], F32, name="tiny_c")
    k.memset(c.tiny_c[:], 1e-30, w=[c.tiny_c])
    c.gneps_c = k.sb([128, 1], F32, name="gneps_c")
    k.memset(c.gneps_c[:], 64e-5, w=[c.gneps_c])


def v3(ap, c=4):
    return ap.rearrange("p (c t) -> p c t", c=c)


CW = 0.6065306597126334
RW_COLS = 1696
EVEN_IN = 3232
TT_ = 256


class EvenProj:
    def __init__(self, k, cx, first):
        self.k, self.cx, self.first = k, cx, first
        sb = k.sb
        self.Wa = [sb([128, RW_COLS], BF16, name="Wa") for _ in range(KT)]
        self.Wb = [sb([128, RW_COLS], BF16, name="Wb") for _ in range(KT)]
        self.Ws = [sb([128, 1536], BF16, name="Ws") for _ in range(KT)]
        self.gain = sb([128, KT], F32, name="mgain")
        self.wa2 = sb([64, 512], BF16, name="wa2")
        self.g2 = sb([96, 512], BF16, name="g2")
        self.v1 = sb([128, KT, 32], BF16, name="v1")
        self.v2 = sb([32, 512], BF16, name="v2")
        self.cols = {nm: sb([128, 4], F32, name="col_" + nm) for nm in
                     ("w0", "a0", "kk", "ka", "rk", "lnw", "lnb", "v0")}
        self.qg = sb([128, 1], F32, name="qg")
        self.kg = sb([128, 1], F32, name="kg")

    def alloc_act(self):
        sb = self.k.sb
        self.xt = [sb([128, KT, TT_ + 1], F32, name="m_xt") for _ in range(1)]
        self.sq = sb([128, KT, TT_ + 1], BF16, name="m_sq")
        self.hT = sb([128, KT, TT_ + 1], BF16, name="m_hT")
        self.rstd = sb([128, TT_ + 1], F32, name="m_rstd")
        self.wad = sb([64, TT_], BF16, name="wad")
        self.sgd = sb([96, TT_], BF16, name="sgd")
        self.hv1 = sb([32, TT_], BF16, name="hv1")
        P2 = range(2)
        f = lambda nm, w=TT_: [sb([128, w], F32, name=nm) for _ in P2]
        self.sgw, self.asig, self.g, self.r, self.t1, self.kkn = f("sgw"), f("asig"), f("g"), f("r"), f("t1"), f("kkn")
        self.kmod, self.bb, self.v, self.Gs, self.Gx = f("kmod"), f("bb"), f("v"), f("Gs"), f("Gx")
        self.eG, self.eGn, self.eGx, self.bon, self.yt = f("eG"), f("eGn"), f("eGx"), f("bon"), f("yt")
        self.tmp = f("tmp")
        self.vf = f("vf")
        self.AR = [sb([128, 4, 192], F32, name="AR") for _ in P2]
        self.Kb = [sb([128, 4, 128], F32, name="Kb") for _ in P2]
        self.Bb = [sb([128, 4, 128], F32, name="Bb") for _ in P2]
        self.Vb = [sb([128, 4, 128], F32, name="Vb") for _ in P2]
        self.dend = [sb([128, 4], F32, name="dend") for _ in P2]
        self.H = [sb([128, 128], F32, name="H") for _ in range(4)]
        self.Hd = [sb([128, 128], F32, name="Hd") for _ in P2]
        self.NA = [sb([128, 192], F32, name="NA") for _ in P2]
        self.KA = [sb([128, 192], F32, name="KA") for _ in P2]
        self.TOK = [sb([128, 384], F32, name="TOK") for _ in P2]
        self.U = [sb([128, 128], F32, name="U") for _ in P2]
        self.Np = [[sb([128, 128], F32, name="Np") for _ in range(6)] for _ in P2]
        self.Ap = [[sb([128, 128], F32, name="Ap") for _ in range(2)] for _ in P2]
        self.yo = [sb([128, TT_], BF16, name="yo") for _ in range(2)]
        self.qn = [sb([128, TT_], BF16, name="qn") for _ in range(2)]
        self.sqf = [sb([128, TT_], F32, name="sqf") for _ in range(2)]
        self.sd = [sb([128, TT_], F32, name="sd") for _ in range(2)]
        self.vtok = [sb([128, 512], BF16, name="vtok") for _ in range(2)]
        for T_ in self.AR + self.Kb + self.Bb + self.Vb:
            self.k.memset(T_[:, :, :], 0.0, w=[T_], eng="gpsimd")

    def load_weights(self, P, e):
        k = self.k
        w_in = P["ev_w_in"][e]
        load_col(k, self.gain, P["mix_norm"][2 * e], D, r=[])
        with k.scope():
            stage = [k.sb([128, RW_COLS], F32, name="wstage") for _ in range(2)]
            mu = k.sb([128, RW_COLS], F32, name="mu_bc")
            omu = k.sb([128, RW_COLS], F32, name="omu_bc")
            k.dma(mu[:, :], P["rw_mu"][e].partition_broadcast(128), r=[], w=[mu])
            k.ts(omu[:, :], mu[:, :], -1.0, 1.0, ALU.mult, ALU.add, r=[mu], w=[omu])
            for kt in range(KT):
                st = stage[kt % 2]
                k.dma(st[:, :], w_in[kt * 128:(kt + 1) * 128, 0:RW_COLS], r=[], w=[st])
                k.stt(self.Wa[kt][:, :], st[:, :], self.gain[:, kt:kt + 1], mu[:, :], ALU.mult, ALU.mult,
                      r=[st, self.gain, mu], w=[self.Wa[kt]])
                k.stt(self.Wb[kt][:, :], st[:, :], self.gain[:, kt:kt + 1], omu[:, :], ALU.mult, ALU.mult,
                      r=[st, self.gain, omu], w=[self.Wb[kt]])
                k.dma(self.Ws[kt][:, :], w_in[kt * 128:(kt + 1) * 128, RW_COLS:EVEN_IN], r=[], w=[self.Ws[kt]],
                      q="gpsimd")
                k.ts(self.Ws[kt][:, :], self.Ws[kt][:, :], self.gain[:, kt:kt + 1], 1.0, ALU.mult, ALU.mult,
                     r=[self.gain, self.Ws[kt]], w=[self.Ws[kt]], eng="gpsimd")
        k.dma(self.wa2[0:32, :], P["rw_w2"][e], r=[], w=[self.wa2], q="gpsimd")
        k.dma(self.wa2[32:64, :], P["rw_a2"][e], r=[], w=[self.wa2], q="gpsimd", append=True)
        k.dma(self.g2[:, :], P["rw_g2"][e], r=[], w=[self.g2], q="gpsimd")
        if not self.first:
            k.dma(self.v1[:, :, :], P["rw_v1"][e - 1].rearrange("(kt p) c -> p kt c", p=128), r=[], w=[self.v1],
                  q="gpsimd")
            for kt in range(KT):
                k.ts(self.v1[:, kt, :], self.v1[:, kt, :], self.gain[:, kt:kt + 1], 1.0, ALU.mult, ALU.mult,
                     r=[self.gain, self.v1], w=[self.v1], eng="gpsimd")
            k.dma(self.v2[:, :], P["rw_v2"][e - 1], r=[], w=[self.v2], q="gpsimd")
            load_col(k, self.cols["v0"], P["rw_v0"][e - 1], 512, r=[])
        for nm, key in (("w0", "rw_w0"), ("a0", "rw_a0"), ("kk", "rw_k_k"), ("ka", "rw_k_a"), ("rk", "rw_r_k"),
                        ("lnw", "rw_ln_w"), ("lnb", "rw_ln_b")):
            load_col(k, self.cols[nm], P[key][e], 512, r=[])
        for h in range(2):
            k.dma(self.qg[h * 64:(h + 1) * 64, 0:1], P["sb_q_gain"][e].rearrange("(p o) -> p o", o=1), r=[],
                  w=[self.qg], append=(h > 0))
            k.dma(self.kg[h * 64:(h + 1) * 64, 0:1], P["sb_k_gain"][e].rearrange("(p o) -> p o", o=1), r=[],
                  w=[self.kg], append=(h > 0))
        k.ts(self.qg[:, :], self.qg[:, :], 0.125, None, ALU.mult, None, r=[self.qg], w=[self.qg])
        self.alloc_act()

    def proj(self, ps, Wlist, c0, c1, lo, ntok, first=True, last=True):
        k = self.k
        for kt in range(KT):
            k.mm(ps[0:c1 - c0, :ntok], Wlist[kt][:, c0:c1], self.hT[:, kt, lo:lo + ntok],
                 r=[Wlist[kt], self.hT], w=[ps], start=(first and kt == 0), stop=(last and kt == KT - 1),
                 signal=(last and kt == KT - 1))

    def proj_rw(self, ps, c0, c1):
        self.proj(ps, self.Wb, c0, c1, 1, TT_, True, False)
        self.proj(ps, self.Wa, c0, c1, 0, TT_, False, True)

    def run(self, src, S, TC, vfirst, sbq, sbk, sbv, mixT):
        k, cx = self.k, self.cx
        N = TT_
        for n in range(TC // N):
            t0 = n * N
            seq_start = (t0 % S == 0)
            xt = self.xt[n % len(self.xt)]
            if seq_start:
                k.memset(xt[:, :, 0:1], 0.0, w=[xt])
                k.dma(xt[:, :, 1:N + 1], src[:, t0:t0 + N].rearrange("(kt p) n -> p kt n", p=128),
                      r=src.bs(t0, t0 + N), w=[xt], append=True)
                for p in range(4):
                    k.memset(self.H[p][:, :], 0.0, w=[self.H[p]])
            else:
                k.dma(xt[:, :, :], src[:, t0 - 1:t0 + N].rearrange("(kt p) n -> p kt n", p=128),
                      r=src.bs(t0 - 1, t0 + N), w=[xt])
            rmsnorm_tile(k, cx, xt, None, self.hT, self.sq, self.rstd, N + 1)
            ps = cx.bank()
            self.proj_rw(ps, 1536, 1600)
            k.act(self.wad[0:32, :], ps[0:32, :N], AF.Tanh, r=[ps], w=[self.wad])
            k.act(self.wad[32:64, :], ps[32:64, :N], AF.Copy, r=[ps], w=[self.wad], append=True)
            ps = cx.bank()
            self.proj_rw(ps, 1600, 1696)
            k.act(self.sgd[:, :], ps[0:96, :N], AF.Sigmoid, r=[ps], w=[self.sgd])
            if not self.first:
                ps = cx.bank()
                for kt in range(KT):
                    k.mm(ps[0:32, :N], self.v1[:, kt, :], self.hT[:, kt, 1:N + 1], r=[self.v1, self.hT], w=[ps],
                         start=(kt == 0), stop=(kt == KT - 1), signal=(kt == KT - 1))
                k.act(self.hv1[:, :], ps[0:32, :N], AF.Copy, r=[ps], w=[self.hv1])
            sbjobs = [(lambda i=i: self.sb_qk(i, n, t0, sbq, sbk)) for i in range(8)] + \
                     [(lambda b_=b_: self.sb_v(b_, n, t0, sbv)) for b_ in range(TT_ // 128)]
            for half in range(2):
                for q in range(2):
                    self.prep(2 * half + q, q, n, t0, vfirst)
                for c in range(4):
                    def extra():
                        if sbjobs:
                            sbjobs.pop(0)()
                    self.chunk(c, half, extra)
                for q in range(2):
                    self.post(2 * half + q, q, n, t0, mixT)
            while sbjobs:
                sbjobs.pop(0)()

    def prep(self, p, q, n, t0, vfirst):
        k, cx = self.k, self.cx
        cols = self.cols
        N = TT_
        cs = slice(p * 128, (p + 1) * 128)
        col = lambda nm: cols[nm][:, p:p + 1]
        ps_r, ps_k, ps_v = cx.bank(), cx.bank(), cx.bank()
        self.proj_rw(ps_r, p * 128, (p + 1) * 128)
        self.proj_rw(ps_k, 512 + p * 128, 512 + (p + 1) * 128)
        self.proj_rw(ps_v, 1024 + p * 128, 1024 + (p + 1) * 128)
        r_, t1, kkn, kmod, bb, v_ = self.r[q], self.t1[q], self.kkn[q], self.kmod[q], self.bb[q], self.v[q]
        sgw, asig, g_, tmp = self.sgw[q], self.asig[q], self.g[q], self.tmp[q]
        k.act(r_[:, :], ps_r[:, :N], AF.Copy, r=[ps_r], w=[r_])
        k.ts(t1[:, :], ps_k[:, :N], col("kk"), None, ALU.mult, None, r=[ps_k, cols["kk"]], w=[t1])
        ps_z = cx.bank()
        k.mm(ps_z[:, :N], self.wa2[32:64, cs], self.wad[32:64, :], r=[self.wa2, self.wad], w=[ps_z])
        k.act(asig[:, :], ps_z[:, :N], AF.Sigmoid, r=[ps_z, cols["a0"]], w=[asig], bias=col("a0"))
        k.ts(tmp[:, :], asig[:, :], -1.0, col("ka"), ALU.add, ALU.mult, r=[asig, cols["ka"]], w=[tmp])
        k.stt(kmod[:, :], tmp[:, :], 1.0, ps_k[:, :N], ALU.add, ALU.mult, r=[tmp, ps_k], w=[kmod])
        vdst = vfirst.b(("p", p, n))
        if self.first:
            k.act(v_[:, :], ps_v[:, :N], AF.Copy, r=[ps_v], w=[v_])
            k.dma(vfirst[p * 128:(p + 1) * 128, t0:t0 + N], v_[:, :], r=[v_], w=[vdst])
        else:
            vf = self.vf[q]
            k.dma(vf[:, :], vfirst[p * 128:(p + 1) * 128, t0:t0 + N], r=[vdst], w=[vf])
            ps_m = cx.bank()
            k.mm(ps_m[:, :N], self.v2[0:32, cs], self.hv1[0:32, :], r=[self.v2, self.hv1], w=[ps_m])
            k.act(tmp[:, :], ps_m[:, :N], AF.Sigmoid, r=[ps_m, cols["v0"]], w=[tmp], bias=col("v0"))
            k.tt(vf[:, :], vf[:, :], ps_v[:, :N], ALU.subtract, r=[vf, ps_v], w=[vf])
            k.tt(vf[:, :], vf[:, :], tmp[:, :], ALU.mult, r=[vf, tmp], w=[vf])
            k.tt(v_[:, :], vf[:, :], ps_v[:, :N], ALU.add, r=[vf, ps_v], w=[v_])
        ps_z = cx.bank()
        k.mm(ps_z[:, :N], self.wa2[0:32, cs], self.wad[0:32, :], r=[self.wa2, self.wad], w=[ps_z])
        k.act(sgw[:, :], ps_z[:, :N], AF.Sigmoid, r=[ps_z, cols["w0"]], w=[sgw], bias=col("w0"))
        ps_z = cx.bank()
        k.mm(ps_z[:, :N], self.g2[0:96, cs], self.sgd[0:96, :], r=[self.g2, self.sgd], w=[ps_z])
        k.act(g_[:, :], ps_z[:, :N], AF.Copy, r=[ps_z], w=[g_])
        k.act(tmp[:, :], t1[:, :], AF.Square, r=[t1], w=[tmp])
        ps_s = cx.bank()
        k.mm(ps_s[:, :N], cx.bones[:, :], tmp[:, :], r=[cx.bones, tmp], w=[ps_s])
        k.act(kkn[:, :], ps_s[:, :N], AF.Sqrt, r=[ps_s, cx.tiny_c], w=[kkn], bias=cx.tiny_c[:, 0:1])
        k.op("vector", lambda e: e.reciprocal(out=kkn[:, :], in_=kkn[:, :]), r=[kkn], w=[kkn])
        k.tt(kkn[:, :], kkn[:, :], t1[:, :], ALU.mult, r=[kkn, t1], w=[kkn])
        k.tt(bb[:, :], kkn[:, :], asig[:, :], ALU.mult, r=[kkn, asig], w=[bb])
        Gs, Gx, eG, eGn, eGx = self.Gs[q], self.Gx[q], self.eG[q], self.eGn[q], self.eGx[q]
        k.op("vector", lambda e: e.tensor_tensor_scan(out=Gs[:, :], data0=cx.cmask[:, :N], data1=sgw[:, :],
                                                      initial=0.0, op0=ALU.mult, op1=ALU.add),
             r=[cx.cmask, sgw], w=[Gs])
        k.tt(Gx[:, :], Gs[:, :], sgw[:, :], ALU.subtract, r=[Gs, sgw], w=[Gx])
        k.act(eG[:, :], Gs[:, :], AF.Exp, r=[Gs], w=[eG], scale=-CW)
        k.act(eGn[:, :], Gs[:, :], AF.Exp, r=[Gs], w=[eGn], scale=CW)
        k.act(eGx[:, :], Gx[:, :], AF.Exp, r=[Gx], w=[eGx], scale=-CW)
        AR, Kb, Bb, Vb = self.AR[q], self.Kb[q], self.Bb[q], self.Vb[q]
        for h in range(2):
            hs = slice(h * 64, (h + 1) * 64)
            ap_ = (h > 0)
            k.stt(AR[hs, :, h * 64:(h + 1) * 64], v3(kkn[hs, :]), -1.0, v3(eGx[hs, :]), ALU.mult, ALU.mult,
                  r=[kkn, eGx], w=[AR], append=ap_)
            k.tt(Kb[hs, :, h * 64:(h + 1) * 64], v3(kmod[hs, :]), v3(eGn[hs, :]), ALU.mult,
                 r=[kmod, eGn], w=[Kb], append=ap_)
            k.tt(Bb[hs, :, h * 64:(h + 1) * 64], v3(bb[hs, :]), v3(eGn[hs, :]), ALU.mult,
                 r=[bb, eGn], w=[Bb], append=ap_, eng="gpsimd")
            k.copy(Vb[hs, :, h * 64:(h + 1) * 64], v3(v_[hs, :]), r=[v_], w=[Vb], append=ap_, eng="gpsimd")
        k.tt(AR[:, :, 128:192], v3(r_[:, :]), v3(eG[:, :]), ALU.mult, r=[r_, eG], w=[AR], append=True)
        k.copy(self.dend[q][:, :], v3(eG[:, :])[:, :, 63], r=[eG], w=[self.dend[q]])
        k.stt(tmp[:, :], r_[:, :], col("rk"), kmod[:, :], ALU.mult, ALU.mult, r=[r_, cols["rk"], kmod], w=[tmp])
        ps_s = cx.bank()
        k.mm(ps_s[:, :N], cx.bones[:, :], tmp[:, :], r=[cx.bones, tmp], w=[ps_s])
        k.tt(self.bon[q][:, :], v_[:, :], ps_s[:, :N], ALU.mult, r=[v_, ps_s], w=[self.bon[q]])

    def chunk(self, c, half, extra=None):
        k, cx = self.k, self.cx
        Q2 = range(2)
        Hs = [self.H[2 * half + q] for q in Q2]
        for q in Q2:
            AR, Kb, Bb, Vb = self.AR[q], self.Kb[q], self.Bb[q], self.Vb[q]
            pA = cx.bank()
            k.mm(pA[:, 0:192], Bb[:, c, :], AR[:, c, :], r=[Bb, AR], w=[pA])
            k.tt(self.NA[q][:, :], pA[:, 0:192], cx.m192[:, :], ALU.mult, r=[pA, cx.m192], w=[self.NA[q]])
            pB = cx.bank()
            k.mm(pB[:, 0:192], Kb[:, c, :], AR[:, c, :], r=[Kb, AR], w=[pB])
            k.tt(self.KA[q][:, :], pB[:, 0:192], cx.m192[:, :], ALU.mult, r=[pB, cx.m192], w=[self.KA[q]])
            pC = cx.bank()
            k.mm(pC[:, 0:128], AR[:, c, 0:128], Bb[:, c, :], r=[AR, Bb], w=[pC])
            k.tt(self.Ap[q][0][:, :], pC[:, 0:128], cx.msl[:, :], ALU.mult, r=[pC, cx.msl], w=[self.Ap[q][0]])
            pT = cx.bank()
            for i, X in enumerate((Kb, Bb, Vb)):
                k.op("tensor", lambda e, X=X, i=i, pT=pT: e.transpose(pT[:, i * 128:(i + 1) * 128], X[:, c, :],
                                                                       cx.ident[:, :]),
                     r=[X, cx.ident], w=[pT], signal=(i == 2))
            k.act(self.TOK[q][:, :], pT[:, 0:384], AF.Copy, r=[pT], w=[self.TOK[q]])
        for q in Q2:
            pw = cx.bank()
            k.mm(pw[:, 0:128], self.AR[q][:, c, 0:128], Hs[q][:, :], r=[self.AR[q], Hs[q]], w=[pw],
                 start=True, stop=False, signal=False)
            k.mm(pw[:, 0:128], self.KA[q][:, 0:128], self.TOK[q][:, 256:384], r=[self.KA[q], self.TOK[q]], w=[pw],
                 start=False, stop=True)
            k.copy(self.U[q][:, :], pw[:, 0:128], r=[pw], w=[self.U[q]])
            k.ts(self.Hd[q][:, :], Hs[q][:, :], self.dend[q][:, c:c + 1], None, ALU.mult, None,
                 r=[Hs[q], self.dend[q]], w=[self.Hd[q]], eng="gpsimd")
        def npw(q, j):
            if j == 0:
                return self.NA[q][:, 0:128], self.NA[q]
            return self.Np[q][j][:, :], self.Np[q][j]

        for j in range(6):
            if j < 5:
                for q in Q2:
                    Ap = self.Ap[q]
                    nj, njb = npw(q, j)
                    pn = cx.bank()
                    k.mm(pn[:, 0:128], Ap[j % 2][:, :], nj, r=[Ap[j % 2], njb], w=[pn])
                    k.act(self.Np[q][j + 1][:, :], pn[:, 0:128], AF.Copy, r=[pn], w=[self.Np[q][j + 1]])
                    if j < 4:
                        pa = cx.bank()
                        k.mm(pa[:, 0:128], nj, Ap[j % 2][:, :], r=[njb, Ap[j % 2]], w=[pa])
                        k.copy(Ap[(j + 1) % 2][:, :], pa[:, 0:128], r=[pa], w=[Ap[(j + 1) % 2]], eng="gpsimd" if False else "vector")
            for q in Q2:
                nj, njb = npw(q, j)
                pu = cx.bank()
                k.mm(pu[:, 0:128], nj, self.U[q][:, :], r=[njb, self.U[q]], w=[pu])
                k.tt(self.U[q][:, :], self.U[q][:, :], pu[:, 0:128], ALU.add, r=[self.U[q], pu], w=[self.U[q]])
            if extra is not None and j in (1, 3):
                extra()
        for q in Q2:
            py = cx.bank()
            k.mm(py[:, 0:64], Hs[q][:, :], self.AR[q][:, c, 128:192], r=[Hs[q], self.AR[q]], w=[py],
                 start=True, stop=False, signal=False)
            k.mm(py[:, 0:64], self.TOK[q][:, 256:384], self.KA[q][:, 128:192], r=[self.TOK[q], self.KA[q]], w=[py],
                 start=False, stop=False, signal=False)
            k.mm(py[:, 0:64], self.U[q][:, :], self.NA[q][:, 128:192], r=[self.U[q], self.NA[q]], w=[py],
                 start=False, stop=True)
            k.act(self.yt[q][:, c * 64:(c + 1) * 64], py[:, 0:64], AF.Copy, r=[py], w=[self.yt[q]], append=(c > 0))
            ph = cx.bank()
            k.mm(ph[:, 0:128], self.TOK[q][:, 0:128], self.TOK[q][:, 256:384], r=[self.TOK[q]], w=[ph],
                 start=True, stop=False, signal=False)
            k.mm(ph[:, 0:128], self.TOK[q][:, 128:256], self.U[q][:, :], r=[self.TOK[q], self.U[q]], w=[ph],
                 start=False, stop=True)
            k.stt(Hs[q][:, :], ph[:, 0:128], self.dend[q][:, c:c + 1], self.Hd[q][:, :], ALU.mult, ALU.add,
                  r=[ph, self.dend[q], self.Hd[q]], w=[Hs[q]])

    def post(self, p, q, n, t0, mixT):
        k, cx = self.k, self.cx
        N = TT_
        yt, tmp, g_, bon = self.yt[q], self.tmp[q], self.g[q], self.bon[q]
        col = lambda nm: self.cols[nm][:, p:p + 1]
        pm = cx.bank()
        k.mm(pm[:, :N], cx.bones[:, :], yt[:, :], r=[cx.bones, yt], w=[pm])
        k.stt(yt[:, :], pm[:, :N], -1.0 / 64, yt[:, :], ALU.mult, ALU.add, r=[pm, yt], w=[yt])
        k.act(tmp[:, :], yt[:, :], AF.Square, r=[yt], w=[tmp])
        pv = cx.bank()
        k.mm(pv[:, :N], cx.bones[:, :], tmp[:, :], r=[cx.bones, tmp], w=[pv])
        k.act(tmp[:, :], pv[:, :N], AF.Sqrt, r=[pv, cx.gneps_c], w=[tmp], scale=1.0 / 64, bias=cx.gneps_c[:, 0:1])
        k.op("vector", lambda e: e.reciprocal(out=tmp[:, :], in_=tmp[:, :]), r=[tmp], w=[tmp])
        k.tt(yt[:, :], yt[:, :], tmp[:, :], ALU.mult, r=[yt, tmp], w=[yt])
        k.ts(yt[:, :], yt[:, :], col("lnw"), col("lnb"), ALU.mult, ALU.add, r=[yt, self.cols["lnw"], self.cols["lnb"]],
             w=[yt])
        k.tt(yt[:, :], yt[:, :], bon[:, :], ALU.add, r=[yt, bon], w=[yt])
        yo = self.yo[q]
        k.tt(yo[:, :], yt[:, :], g_[:, :], ALU.mult, r=[yt, g_], w=[yo])
        k.dma(mixT[p * 128:(p + 1) * 128, t0:t0 + N], yo[:, :], r=[yo], w=[mixT.b(("rw", p, n))])

    def sb_qk(self, i, n, t0, sbq, sbk):
        k, cx = self.k, self.cx
        N = TT_
        ps = cx.bank()
        self.proj(ps, self.Ws, i * 128, (i + 1) * 128, 1, N)
        sqf, sd, qn = self.sqf[i % 2], self.sd[i % 2], self.qn[i % 2]
        k.act(sqf[:, :], ps[:, :N], AF.Square, r=[ps], w=[sqf])
        pss = cx.bank()
        k.mm(pss[:, :N], cx.bones[:, :], sqf[:, :], r=[cx.bones, sqf], w=[pss])
        k.act(sd[:, :], pss[:, :N], AF.Sqrt, r=[pss, cx.eps], w=[sd], scale=1.0 / 64, bias=cx.eps[:, 0:1])
        k.op("vector", lambda e, sd=sd: e.reciprocal(out=sd[:, :], in_=sd[:, :]), r=[sd], w=[sd])
        gcol = self.qg if i < 4 else self.kg
        k.stt(qn[:, :], ps[:, :N], gcol[:, 0:1], sd[:, :], ALU.mult, ALU.mult, r=[ps, gcol, sd], w=[qn])
        dst = sbq if i < 4 else sbk
        j = i % 4
        k.dma(dst[j * 128:(j + 1) * 128, t0:t0 + N], qn[:, :], r=[qn], w=[dst.b((j, n))])

    def sb_v(self, blk, n, t0, sbv):
        k, cx = self.k, self.cx
        ps = cx.bank()
        for kt in range(KT):
            k.mm(ps[:, 0:512], self.hT[:, kt, 1 + blk * 128:1 + (blk + 1) * 128], self.Ws[kt][:, 1024:1536],
                 r=[self.hT, self.Ws[kt]], w=[ps], start=(kt == 0), stop=(kt == KT - 1), signal=(kt == KT - 1))
        vt = self.vtok[blk % 2]
        k.act(vt[:, :], ps[:, 0:512], AF.Copy, r=[ps], w=[vt])
        k.dma(sbv[t0 + blk * 128:t0 + (blk + 1) * 128, :], vt[:, :], r=[vt], w=[sbv.b((n, blk))])


class SBAttn:
    def __init__(self, k, cx, S):
        self.k, self.cx, self.S = k, cx, S
        sb = k.sb
        self.Kt = sb([64, S], BF16, name="sbK")
        self.Vt = sb([128, S // 128, 512], BF16, name="sbV")
        self.Q = [sb([64, 512], BF16, name="sbQ") for _ in range(2)]
        self.e = [sb([128, 512], F32, name="sb_e") for _ in range(2)]
        self.P = [sb([128, 512], F32, name="sb_P") for _ in range(2)]
        self.u = [sb([128, 512], F32, name="sb_u") for _ in range(2)]
        self.wg = [sb([128, 512], BF16, name="sb_w") for _ in range(2)]
        self.Pacc = sb([128, 512], F32, name="sb_Pacc")
        self.mask = sb([128, 4, 512], F32, name="sb_mask")
        self.triu = sb([128, 128], F32, name="sb_triu")
        self.osb = [sb([64, 512], BF16, name="sb_o") for _ in range(2)]
        for rel in range(4):
            k.op("gpsimd", lambda e, rel=rel: e.affine_select(out=self.mask[:, rel, :],
                                                              in_=cx.ones_f[:, 0:1].to_broadcast([128, 512]),
                                                              pattern=[[1, 512]], compare_op=ALU.is_gt, fill=0.0,
                                                              base=-rel * 128, channel_multiplier=-1),
                 r=[cx.ones_f], w=[self.mask], append=(rel > 0))
        k.op("gpsimd", lambda e: e.affine_select(out=self.triu[:, :], in_=cx.ones_f[:, :], pattern=[[-1, 128]],
                                                 compare_op=ALU.is_gt, fill=0.0, base=0, channel_multiplier=1),
             r=[cx.ones_f], w=[self.triu])

    def run(self, TC, sbq, sbk, sbv, mixT):
        k, cx, S = self.k, self.cx, self.S
        cx.nrot = 7
        it = 0
        for b in range(TC // S):
            nb8 = max(1, (S // 128) // 8)
            for i8 in range(nb8):
                bl = slice(i8 * 8, min((i8 + 1) * 8, S // 128))
                r0 = b * S + i8 * 8 * 128
                r1 = min(r0 + 8 * 128, (b + 1) * S)
                k.dma(self.Vt[:, bl, :], sbv[r0:r1, :].rearrange("(blk p) d -> p blk d", p=128),
                      r=[], w=[self.Vt], append=(i8 > 0))
            for h in range(8):
                k.dma(self.Kt[:, :], sbk[h * 64:(h + 1) * 64, b * S:(b + 1) * S], r=[], w=[self.Kt])
                for qt in range(S // 512):
                    Q = self.Q[qt % 2]
                    c0 = b * S + qt * 512
                    k.dma(Q[:, :], sbq[h * 64:(h + 1) * 64, c0:c0 + 512], r=[], w=[Q])
                    po = cx.fixed(7)
                    nkb = (qt + 1) * 4
                    for idx, kb in enumerate(range(nkb - 1, -1, -1)):
                        rel = kb - qt * 4
                        e_, P_, u_, wg = self.e[it % 2], self.P[it % 2], self.u[it % 2], self.wg[it % 2]
                        it += 1
                        pz = cx.bank()
                        k.mm(pz[:, :512], self.Kt[:, kb * 128:(kb + 1) * 128], Q[:, :], r=[self.Kt, Q], w=[pz])
                        k.act(e_[:, :], pz[:, :512], AF.Exp, r=[pz], w=[e_])
                        k.act(P_[:, :], e_[:, :], AF.Ln, r=[e_, cx.one_c], w=[P_], bias=cx.one_c[:, 0:1])
                        if rel >= 0:
                            k.tt(P_[:, :], P_[:, :], self.mask[:, rel, :], ALU.mult, r=[P_, self.mask], w=[P_],
                                 eng="gpsimd")
                        pst = cx.bank()
                        k.mm(pst[:, :512], self.triu[:, :], P_[:, :], r=[self.triu, P_], w=[pst], start=True,
                             stop=(idx == 0), signal=(idx == 0))
                        if idx > 0:
                            k.mm(pst[:, :512], cx.ones_f[:, :], self.Pacc[:, :], r=[cx.ones_f, self.Pacc], w=[pst],
                                 start=False, stop=True)
                        k.tt(u_[:, :], pz[:, :512], P_[:, :], ALU.subtract, r=[pz, P_], w=[u_])
                        k.tt(u_[:, :], u_[:, :], pst[:, :512], ALU.subtract, r=[u_, pst], w=[u_])
                        k.act(wg[:, :], u_[:, :], AF.Exp, r=[u_], w=[wg])
                        if rel >= 0:
                            k.tt(wg[:, :], wg[:, :], self.mask[:, rel, :], ALU.mult, r=[wg, self.mask], w=[wg])
                        k.mm(po[0:64, :512], self.Vt[:, kb, h * 64:(h + 1) * 64], wg[:, :], r=[self.Vt, wg], w=[po],
                             start=(idx == 0), stop=(idx == nkb - 1), signal=True)
                        if idx == 0:
                            k.copy(self.Pacc[:, :], P_[:, :], r=[P_], w=[self.Pacc], eng="gpsimd")
                        elif idx < nkb - 1:
                            k.tt(self.Pacc[:, :], self.Pacc[:, :], P_[:, :], ALU.add, r=[self.Pacc, P_], w=[self.Pacc],
                                 eng="gpsimd")
                    osb = self.osb[qt % 2]
                    k.act(osb[:, :], po[0:64, :512], AF.Copy, r=[po], w=[osb])
                    k.dma(mixT[512 + h * 64:512 + (h + 1) * 64, c0:c0 + 512], osb[:, :], r=[osb],
                          w=[mixT.b(("sb", b, h, qt))])
        cx.nrot = 8


class OutProj:
    def __init__(self, k, cx):
        self.k, self.cx = k, cx
        self.Wo = [k.sb([128, D], BF16, name="Wo") for _ in range(KT)]
        self.xt = [k.sb([128, KT, NT], F32, name="o_xt") for _ in range(2)]
        self.mt = [k.sb([128, KT, NT], BF16, name="o_mt") for _ in range(2)]

    def load_weights(self, w_out_ap):
        for kt in range(KT):
            self.k.dma(self.Wo[kt][:, :], w_out_ap[kt * 128:(kt + 1) * 128, :], r=[], w=[self.Wo[kt]], q="gpsimd")

    def run(self, src, dst, mixT, TC):
        k, cx = self.k, self.cx
        for n in range(TC // NT):
            t0 = n * NT
            xt, mt = self.xt[n % 2], self.mt[n % 2]
            k.dma(xt[:, :, :], src[:, t0:t0 + NT].rearrange("(kt p) n -> p kt n", p=128), r=src.bs(t0, t0 + NT), w=[xt])
            k.dma(mt[:, :, :], mixT[:, t0:t0 + NT].rearrange("(kt p) n -> p kt n", p=128), r=[], w=[mt])
            for m in range(KT):
                po = cx.bank()
                for kt in range(KT):
                    k.mm(po[:, :NT], self.Wo[kt][:, m * 128:(m + 1) * 128], mt[:, kt, :], r=[self.Wo[kt], mt], w=[po],
                         start=(kt == 0), stop=(kt == KT - 1), signal=(kt == KT - 1))
                k.tt(xt[:, m, :], xt[:, m, :], po[:, :NT], ALU.add, r=[xt, po], w=[xt])
            k.dma(dst[:, t0:t0 + NT].rearrange("(kt p) n -> p kt n", p=128), xt[:, :, :], r=[xt], w=dst.bs(t0, t0 + NT))


def even_mixer_layer(k, cx, P, e, src, dst, S, TC, scr):
    with k.scope():
        ep = EvenProj(k, cx, first=(e == 0))
        ep.load_weights(P, e)
        ep.run(src, S, TC, scr["vfirst"], scr["sbq"], scr["sbk"], scr["sbv"], scr["mixT"])
    with k.scope():
        sa = SBAttn(k, cx, S)
        sa.run(TC, scr["sbq"], scr["sbk"], scr["sbv"], scr["mixT"])
    with k.scope():
        op = OutProj(k, cx)
        op.load_weights(P["ev_w_out"][e])
        op.run(src, dst, scr["mixT"], TC)


ODD_IN = 3328
PI = 3.141592653589793


class OddProj:
    def __init__(self, k, cx, o):
        self.k, self.cx, self.o = k, cx, o
        sb = k.sb
        self.W = [sb([128, ODD_IN], BF16, name="Wodd") for _ in range(KT)]
        self.gain = sb([128, KT], F32, name="ogain")
        self.lb = sb([128, 6], F32, name="lb")
        self.oml = sb([128, 6], F32, name="oml")
        self.l0 = sb([128, 6], F32, name="l0")
        self.l1 = sb([128, 6], F32, name="l1")
        self.hgg = sb([128, 6], F32, name="hgg")

    def load_weights(self, P):
        k, o = self.k, self.o
        load_col(k, self.gain, P["mix_norm"][2 * o + 1], D, r=[])
        w_in = P["od_w_in"][o]
        CH = 1664
        for kt in range(KT):
            for c in range(2):
                k.dma(self.W[kt][:, c * CH:(c + 1) * CH], w_in[kt * 128:(kt + 1) * 128, c * CH:(c + 1) * CH], r=[],
                      w=[self.W[kt]], q="gpsimd", append=(c > 0))
            k.ts(self.W[kt][:, :], self.W[kt][:, :], self.gain[:, kt:kt + 1], 1.0, ALU.mult, ALU.mult,
                 r=[self.gain, self.W[kt]], w=[self.W[kt]], eng="gpsimd")
        load_col(k, self.hgg, P["hg_gain"][o], 768, r=[])
        if o == 0:
            k.memset(self.lb[:, :], 0.0, w=[self.lb])
        else:
            load_col(k, self.l0, P["hg_lb"][0], 768, r=[])
            load_col(k, self.l1, P["hg_lb"][1], 768, r=[])
            k.tt(self.l1[:, :], self.l1[:, :], self.l0[:, :], ALU.subtract, r=[self.l1, self.l0], w=[self.l1])
            k.act(self.lb[:, :], self.l1[:, :], AF.Sigmoid, r=[self.l1], w=[self.lb])
        k.ts(self.oml[:, :], self.lb[:, :], -1.0, 1.0, ALU.mult, ALU.add, r=[self.lb], w=[self.oml])
        sb = k.sb
        N = TT_
        self.xt = sb([128, KT, N], F32, name="od_xt")
        self.sq = sb([128, KT, N], BF16, name="od_sq")
        self.hT = sb([128, KT, N], BF16, name="od_hT")
        self.rstd = sb([128, N], F32, name="od_rstd")
        self.us = [sb([128, N], F32, name="od_u") for _ in range(2)]
        H6 = range(6)
        f = lambda nm: [sb([128, N], F32, name=nm) for _ in H6]
        self.Qt, self.Kt, self.It, self.sg, self.ot = f("Qt"), f("Kt"), f("It"), f("sgate"), f("ot")
        self.fs = [sb([128, N], F32, name="fs") for _ in range(2)]
        self.G = [sb([128, N], F32, name="Gh") for _ in range(2)]
        self.eG = [sb([128, N], F32, name="eGh") for _ in range(2)]
        self.eGn = [sb([128, N], F32, name="eGnh") for _ in range(2)]
        self.dend = [sb([128, 4], F32, name="dendh") for _ in H6]
        self.H = [sb([128, 128], F32, name="Hh") for _ in H6]
        self.Hd = [sb([128, 128], F32, name="Hdh") for _ in H6]
        self.AttT = [sb([64, 64], F32, name="AttT") for _ in H6]
        self.TOK = [sb([64, 256], F32, name="TOKh") for _ in H6]
        self.yo = [sb([128, N], BF16, name="yoh") for _ in range(2)]
        self.tmp = [sb([128, N], F32, name="tmph") for _ in range(2)]

    def proj(self, ps, c0, c1):
        k = self.k
        for kt in range(KT):
            k.mm(ps[0:c1 - c0, :TT_], self.W[kt][:, c0:c1], self.hT[:, kt, :], r=[self.W[kt], self.hT], w=[ps],
                 start=(kt == 0), stop=(kt == KT - 1), signal=(kt == KT - 1))

    def run(self, src, S, TC, s5u, mixT):
        k, cx = self.k, self.cx
        N = TT_
        for n in range(TC // N):
            t0 = n * N
            xt = self.xt
            k.dma(xt[:, :, :], src[:, t0:t0 + N].rearrange("(kt p) n -> p kt n", p=128), r=src.bs(t0, t0 + N), w=[xt])
            if t0 % S == 0:
                for hd in range(6):
                    k.memset(self.H[hd][:, :], 0.0, w=[self.H[hd]])
            rmsnorm_tile(k, cx, xt, None, self.hT, self.sq, self.rstd, N)
            for i in range(2):
                ps = cx.bank()
                self.proj(ps, i * 128, (i + 1) * 128)
                us = self.us[i]
                k.act(us[:, :], ps[:, :N], AF.Copy, r=[ps], w=[us])
                k.dma(s5u[i * 128:(i + 1) * 128, t0:t0 + N], us[:, :], r=[us], w=[s5u.b((i, n))])
            for hd in range(6):
                q2 = hd % 2
                ps_q, ps_f, ps_i, ps_g = cx.bank(), cx.bank(), cx.bank(), cx.bank()
                self.proj(ps_f, 1024 + hd * 128, 1024 + (hd + 1) * 128)
                self.proj(ps_q, 256 + hd * 128, 256 + (hd + 1) * 128)
                self.proj(ps_i, 1792 + hd * 128, 1792 + (hd + 1) * 128)
                self.proj(ps_g, 2560 + hd * 128, 2560 + (hd + 1) * 128)
                fs, G, eG, eGn = self.fs[q2], self.G[q2], self.eG[q2], self.eGn[q2]
                k.act(fs[:, :], ps_f[:, :N], AF.Sigmoid, r=[ps_f], w=[fs])
                k.ts(fs[:, :], fs[:, :], self.oml[:, hd:hd + 1], self.lb[:, hd:hd + 1], ALU.mult, ALU.add,
                     r=[fs, self.oml, self.lb], w=[fs])
                k.act(G[:, :], fs[:, :], AF.Ln, r=[fs], w=[G])
                k.op("vector", lambda e, G=G: e.tensor_tensor_scan(out=G[:, :], data0=cx.cmask[:, :N], data1=G[:, :],
                                                                    initial=0.0, op0=ALU.mult, op1=ALU.add),
                     r=[cx.cmask, G], w=[G])
                k.act(eG[:, :], G[:, :], AF.Exp, r=[G], w=[eG])
                k.act(eGn[:, :], G[:, :], AF.Exp, r=[G], w=[eGn], scale=-1.0)
                k.tt(self.Qt[hd][:, :], ps_q[:, :N], eG[:, :], ALU.mult, r=[ps_q, eG], w=[self.Qt[hd]])
                k.ts(fs[:, :], fs[:, :], -1.0, 1.0, ALU.mult, ALU.add, r=[fs], w=[fs])
                k.tt(self.Kt[hd][:, :], fs[:, :], eGn[:, :], ALU.mult, r=[fs, eGn], w=[self.Kt[hd]], eng="gpsimd")
                k.act(self.It[hd][:, :], ps_i[:, :N], AF.Copy, r=[ps_i], w=[self.It[hd]])
                k.act(self.sg[hd][:, :], ps_g[:, :N], AF.Silu, r=[ps_g], w=[self.sg[hd]])
                k.copy(self.dend[hd][:, :], v3(eG[:, :])[:, :, 63], r=[eG], w=[self.dend[hd]])
            for c in range(4):
                self.chunk(c)
            for hd in range(6):
                self.post(hd, n, t0, mixT)

    def chunk(self, c):
        k, cx = self.k, self.cx
        cs = slice(c * 64, (c + 1) * 64)
        for hd in range(6):
            Qt, Kt, It, H = self.Qt[hd], self.Kt[hd], self.It[hd], self.H[hd]
            pA = cx.bank()
            k.mm(pA[0:64, 0:64], Kt[:, cs], Qt[:, cs], r=[Kt, Qt], w=[pA])
            k.tt(self.AttT[hd][:, :], pA[0:64, 0:64], cx.m192[0:64, 128:192], ALU.mult, r=[pA, cx.m192],
                 w=[self.AttT[hd]])
            pT = cx.bank()
            k.op("tensor", lambda e, pT=pT, Kt=Kt: e.transpose(pT[0:64, 0:128], Kt[:, cs], cx.ident[:, :]),
                 r=[Kt, cx.ident], w=[pT], signal=False)
            k.op("tensor", lambda e, pT=pT, It=It: e.transpose(pT[0:64, 128:256], It[:, cs], cx.ident[:, :]),
                 r=[It, cx.ident], w=[pT], signal=True)
            k.act(self.TOK[hd][:, :], pT[0:64, 0:256], AF.Copy, r=[pT], w=[self.TOK[hd]])
            k.ts(self.Hd[hd][:, :], H[:, :], self.dend[hd][:, c:c + 1], None, ALU.mult, None,
                 r=[H, self.dend[hd]], w=[self.Hd[hd]], eng="gpsimd")
        for hd in range(6):
            Qt, H, TOK = self.Qt[hd], self.H[hd], self.TOK[hd]
            pO = cx.bank()
            k.mm(pO[:, 0:64], H[:, :], Qt[:, cs], r=[H, Qt], w=[pO], start=True, stop=False, signal=False)
            k.mm(pO[:, 0:64], TOK[:, 128:256], self.AttT[hd][:, :], r=[TOK, self.AttT[hd]], w=[pO],
                 start=False, stop=True)
            k.act(self.ot[hd][:, cs], pO[:, 0:64], AF.Copy, r=[pO], w=[self.ot[hd]], append=(c > 0))
            pH = cx.bank()
            k.mm(pH[:, 0:128], TOK[:, 0:128], TOK[:, 128:256], r=[TOK], w=[pH])
            k.stt(H[:, :], pH[:, 0:128], self.dend[hd][:, c:c + 1], self.Hd[hd][:, :], ALU.mult, ALU.add,
                  r=[pH, self.dend[hd], self.Hd[hd]], w=[H])

    def post(self, hd, n, t0, mixT):
        k, cx = self.k, self.cx
        N = TT_
        ot, tmp = self.ot[hd], self.tmp[hd % 2]
        k.act(tmp[:, :], ot[:, :], AF.Square, r=[ot], w=[tmp])
        pss = cx.bank()
        k.mm(pss[:, :N], cx.ones_f[:, :], tmp[:, :], r=[cx.ones_f, tmp], w=[pss])
        k.act(tmp[:, :], pss[:, :N], AF.Sqrt, r=[pss, cx.eps], w=[tmp], scale=1.0 / 128, bias=cx.eps[:, 0:1])
        k.op("vector", lambda e: e.reciprocal(out=tmp[:, :], in_=tmp[:, :]), r=[tmp], w=[tmp])
        k.tt(ot[:, :], ot[:, :], tmp[:, :], ALU.mult, r=[ot, tmp], w=[ot])
        yo = self.yo[hd % 2]
        k.stt(yo[:, :], ot[:, :], self.hgg[:, hd:hd + 1], self.sg[hd][:, :], ALU.mult, ALU.mult,
              r=[ot, self.hgg, self.sg[hd]], w=[yo])
        k.dma(mixT[256 + hd * 128:256 + (hd + 1) * 128, t0:t0 + N], yo[:, :], r=[yo], w=[mixT.b(("hg", hd, n))])


class S5Stage:
    def __init__(self, k, cx, o, S):
        self.k, self.cx, self.o, self.S = k, cx, o, S
        sb = k.sb
        f8 = lambda nm: sb([128, 8], F32, name=nm)
        self.ar, self.ai, self.dt, self.rho, self.th, self.thr = f8("ar"), f8("ai"), f8("dt"), f8("rho"), f8("th"), f8("thr")
        self.cs, self.sn, self.nsn, self.c1r, self.c1i, self.nc1i = f8("cs"), f8("sn"), f8("nsn"), f8("c1r"), f8("c1i"), f8("nc1i")
        self.t8 = [f8("t8") for _ in range(4)]
        self.halfpi = sb([128, 1], F32, name="halfpi")
        self.ldb = sb([128, 16], F32, name="ldb")
        self.tabc = sb([128, S], F32, name="tabc")
        self.tabs = sb([128, S], F32, name="tabs")
        self.tt1 = sb([128, max(S // 2, 512)], F32, name="tt1")
        self.bre = sb([128, 16], F32, name="bre")
        self.bim = sb([128, 16], F32, name="bim")
        self.Bblk = [sb([128, 32], F32, name="Bblk") for _ in range(2)]
        self.Bl = [sb([32, 128], F32, name="Bl") for _ in range(2)]
        self.Cnat = [sb([32, 128], F32, name="Cnat") for _ in range(2)]
        self.Cl = [sb([128, 32], F32, name="Cl") for _ in range(2)]
        self.dcol = sb([32, 1], F32, name="dcol")
        self.cm = [sb([128, 1], F32, name="cm") for _ in range(4)]
        self.ut = [sb([32, 512], F32, name="ut") for _ in range(2)]
        f5 = lambda nm: sb([128, 512], F32, name=nm)
        self.ta, self.tb, self.br, self.bi, self.wr, self.wi, self.xr, self.xi = [f5(n_) for n_ in
                                                                                   ("ta", "tb", "br", "bi", "wr", "wi", "xr", "xi")]
        self.wr2, self.wi2 = f5("wr2"), f5("wi2")
        self.yv = [sb([32, 512], F32, name="yv") for _ in range(2)]
        self.y2 = sb([32, 512], F32, name="y2")

    def exp_taylor(self, dst, src, nsq, deg):
        import math
        k = self.k
        y, s_ = self.t8[2], self.t8[3]
        k.ts(y[:, :], src[:, :], 1.0 / (1 << nsq), None, ALU.mult, None, r=[src], w=[y])
        k.ts(s_[:, :], y[:, :], 1.0 / math.factorial(deg), None, ALU.mult, None, r=[y], w=[s_])
        for kk in range(deg - 1, 0, -1):
            k.stt(s_[:, :], s_[:, :], 1.0 / math.factorial(kk), y[:, :], ALU.add, ALU.mult, r=[s_, y], w=[s_])
        k.ts(dst[:, :], s_[:, :], 1.0, None, ALU.add, None, r=[s_], w=[dst])
        for _ in range(nsq):
            k.tt(dst[:, :], dst[:, :], dst[:, :], ALU.mult, r=[dst], w=[dst])

    def prep_scalars(self, P):
        k, o = self.k, self.o
        k.memset(self.halfpi[:, :], PI / 2, w=[self.halfpi])
        load_col(k, self.ar, P["s5_a_re"][o].rearrange("a p -> (a p)"), 1024)
        load_col(k, self.ai, P["s5_a_im"][o].rearrange("a p -> (a p)"), 1024)
        ldb = self.ldb
        k.dma(ldb[:, :], P["s5_log_dt"][o].partition_broadcast(128), r=[], w=[ldb])
        ldv = ldb[:, :].rearrange("p (j g) -> p j g", g=2)
        for g in range(2):
            k.copy(self.dt[g * 64:(g + 1) * 64, :], ldv[g * 64:(g + 1) * 64, :, g], r=[ldb], w=[self.dt], append=(g > 0))
        self.exp_taylor(self.dt, self.dt, nsq=3, deg=13)
        k.ts(self.ar[:, :], self.ar[:, :], -1e-4, None, ALU.min, None, r=[self.ar], w=[self.ar])
        k.tt(self.rho[:, :], self.ar[:, :], self.dt[:, :], ALU.mult, r=[self.ar, self.dt], w=[self.rho])
        self.exp_taylor(self.rho, self.rho, nsq=2, deg=12)
        k.tt(self.th[:, :], self.ai[:, :], self.dt[:, :], ALU.mult, r=[self.ai, self.dt], w=[self.th])
        t0_, t1_ = self.t8[0], self.t8[1]
        k.copy(self.thr[:, :], self.th[:, :], r=[self.th], w=[self.thr])
        for m in (1, 3, 5, 7, 9, 11):
            k.ts(t0_[:, :], self.th[:, :], m * PI, 2 * PI, ALU.is_gt, ALU.mult, r=[self.th], w=[t0_])
            k.tt(self.thr[:, :], self.thr[:, :], t0_[:, :], ALU.subtract, r=[self.thr, t0_], w=[self.thr])
        k.act(self.sn[:, :], self.thr[:, :], AF.Sin, r=[self.thr], w=[self.sn])
        k.ts(t1_[:, :], self.thr[:, :], PI / 2, None, ALU.add, None, r=[self.thr], w=[t1_])
        k.ts(t0_[:, :], t1_[:, :], PI, 2 * PI, ALU.is_gt, ALU.mult, r=[t1_], w=[t0_])
        k.tt(t1_[:, :], t1_[:, :], t0_[:, :], ALU.subtract, r=[t1_, t0_], w=[t1_])
        k.act(self.cs[:, :], t1_[:, :], AF.Sin, r=[t1_], w=[self.cs])
        k.ts(self.nsn[:, :], self.sn[:, :], -1.0, None, ALU.mult, None, r=[self.sn], w=[self.nsn])
        nr, ni, den, t3 = self.t8[0], self.t8[1], self.t8[2], self.t8[3]
        k.tt(nr[:, :], self.rho[:, :], self.cs[:, :], ALU.mult, r=[self.rho, self.cs], w=[nr])
        k.ts(nr[:, :], nr[:, :], -1.0, None, ALU.add, None, r=[nr], w=[nr])
        k.tt(ni[:, :], self.rho[:, :], self.sn[:, :], ALU.mult, r=[self.rho, self.sn], w=[ni])
        k.tt(den[:, :], self.ar[:, :], self.ar[:, :], ALU.mult, r=[self.ar], w=[den])
        k.tt(t3[:, :], self.ai[:, :], self.ai[:, :], ALU.mult, r=[self.ai], w=[t3])
        k.tt(den[:, :], den[:, :], t3[:, :], ALU.add, r=[den, t3], w=[den])
        k.op("vector", lambda e: e.reciprocal(out=den[:, :], in_=den[:, :]), r=[den], w=[den])
        k.tt(self.c1r[:, :], nr[:, :], self.ar[:, :], ALU.mult, r=[nr, self.ar], w=[self.c1r])
        k.tt(t3[:, :], ni[:, :], self.ai[:, :], ALU.mult, r=[ni, self.ai], w=[t3])
        k.tt(self.c1r[:, :], self.c1r[:, :], t3[:, :], ALU.add, r=[self.c1r, t3], w=[self.c1r])
        k.tt(self.c1r[:, :], self.c1r[:, :], den[:, :], ALU.mult, r=[self.c1r, den], w=[self.c1r])
        k.tt(self.c1i[:, :], ni[:, :], self.ar[:, :], ALU.mult, r=[ni, self.ar], w=[self.c1i])
        k.tt(t3[:, :], nr[:, :], self.ai[:, :], ALU.mult, r=[nr, self.ai], w=[t3])
        k.tt(self.c1i[:, :], self.c1i[:, :], t3[:, :], ALU.subtract, r=[self.c1i, t3], w=[self.c1i])
        k.tt(self.c1i[:, :], self.c1i[:, :], den[:, :], ALU.mult, r=[self.c1i, den], w=[self.c1i])
        k.ts(self.nc1i[:, :], self.c1i[:, :], -1.0, None, ALU.mult, None, r=[self.c1i], w=[self.nc1i])

    def prep_tile(self, P, j):
        k, cx, o, S = self.k, self.cx, self.o, self.S
        col = lambda T_: T_[:, j:j + 1]
        k.dma(self.bre[:, :], P["s5_b_re"][o][2 * j:2 * j + 2].rearrange("g p c -> (g p) c"), r=[], w=[self.bre])
        k.dma(self.bim[:, :], P["s5_b_im"][o][2 * j:2 * j + 2].rearrange("g p c -> (g p) c"), r=[], w=[self.bim])
        for T_ in self.Bblk:
            k.memset(T_[:, :], 0.0, w=[T_])
        for g in range(2):
            gs = slice(g * 64, (g + 1) * 64)
            gc = slice(g * 16, (g + 1) * 16)
            k.ts(self.Bblk[0][gs, gc], self.bre[gs, :], self.c1r[gs, j:j + 1], None, ALU.mult, None,
                 r=[self.bre, self.c1r], w=[self.Bblk[0]])
            k.stt(self.Bblk[0][gs, gc], self.bim[gs, :], self.nc1i[gs, j:j + 1], self.Bblk[0][gs, gc], ALU.mult, ALU.add,
                  r=[self.bim, self.nc1i, self.Bblk[0]], w=[self.Bblk[0]])
            k.ts(self.Bblk[1][gs, gc], self.bim[gs, :], self.c1r[gs, j:j + 1], None, ALU.mult, None,
                 r=[self.bim, self.c1r], w=[self.Bblk[1]])
            k.stt(self.Bblk[1][gs, gc], self.bre[gs, :], self.c1i[gs, j:j + 1], self.Bblk[1][gs, gc], ALU.mult, ALU.add,
                  r=[self.bre, self.c1i, self.Bblk[1]], w=[self.Bblk[1]])
        for i in range(2):
            pt = cx.bank()
            k.op("tensor", lambda e, pt=pt, i=i: e.transpose(pt[0:32, 0:128], self.Bblk[i][:, :], cx.ident[:, :]),
                 r=[self.Bblk[i], cx.ident], w=[pt])
            k.act(self.Bl[i][:, :], pt[0:32, 0:128], AF.Copy, r=[pt], w=[self.Bl[i]])
        for i, nm in enumerate(("s5_c_re", "s5_c_im")):
            k.memset(self.Cnat[i][:, :], 0.0, w=[self.Cnat[i]])
            for g in range(2):
                k.dma(self.Cnat[i][g * 16:(g + 1) * 16, g * 64:(g + 1) * 64], P[nm][o][2 * j + g], r=[],
                      w=[self.Cnat[i]], append=(g > 0))
            pt = cx.bank()
            k.op("tensor", lambda e, pt=pt, i=i: e.transpose(pt[:, 0:32], self.Cnat[i][:, :], cx.ident[0:32, 0:32]),
                 r=[self.Cnat[i], cx.ident], w=[pt])
            k.act(self.Cl[i][:, :], pt[:, 0:32], AF.Copy, r=[pt], w=[self.Cl[i]], scale=(1.0 if i == 0 else -1.0))
        k.dma(self.dcol[:, 0:1], P["s5_d"][o][32 * j:32 * j + 32].rearrange("(p o) -> p o", o=1), r=[], w=[self.dcol])
        tc_, ts_ = self.tabc, self.tabs
        k.memset(tc_[:, 0:1], 1.0, w=[tc_])
        k.memset(ts_[:, 0:1], 0.0, w=[ts_])
        cm, sm, nsm = self.cm[0], self.cm[1], self.cm[2]
        k.copy(cm[:, :], col(self.cs), r=[self.cs], w=[cm])
        k.copy(sm[:, :], col(self.sn), r=[self.sn], w=[sm])
        L = 1
        while L < S:
            k.ts(nsm[:, :], sm[:, :], -1.0, None, ALU.mult, None, r=[sm], w=[nsm])
            t1 = self.tt1
            k.ts(t1[:, 0:L], tc_[:, 0:L], cm[:, 0:1], None, ALU.mult, None, r=[tc_, cm], w=[t1])
            k.stt(tc_[:, L:2 * L], ts_[:, 0:L], nsm[:, 0:1], t1[:, 0:L], ALU.mult, ALU.add, r=[ts_, nsm, t1], w=[tc_])
            k.ts(t1[:, 0:L], tc_[:, 0:L], sm[:, 0:1], None, ALU.mult, None, r=[tc_, sm], w=[t1])
            k.stt(ts_[:, L:2 * L], ts_[:, 0:L], cm[:, 0:1], t1[:, 0:L], ALU.mult, ALU.add, r=[ts_, cm, t1], w=[ts_])
            L *= 2
            if L < S:
                e1 = 2 * L // 2 - 1
                t8a = self.cm[3]
                k.ts(t8a[:, :], tc_[:, e1:e1 + 1], col(self.cs), None, ALU.mult, None, r=[tc_, self.cs], w=[t8a])
                k.stt(cm[:, :], ts_[:, e1:e1 + 1], col(self.nsn), t8a[:, :], ALU.mult, ALU.add,
                      r=[ts_, self.nsn, t8a], w=[cm])
                k.ts(t8a[:, :], tc_[:, e1:e1 + 1], col(self.sn), None, ALU.mult, None, r=[tc_, self.sn], w=[t8a])
                k.stt(sm[:, :], ts_[:, e1:e1 + 1], col(self.cs), t8a[:, :], ALU.mult, ALU.add,
                      r=[ts_, self.cs, t8a], w=[sm])

    def run(self, P, TC, s5u, s5y, dbg=None):
        k, cx, S = self.k, self.cx, self.S
        self.dbg = dbg
        self.prep_scalars(P)
        CH = 512
        it = 0
        for j in range(8):
            self.prep_tile(P, j)
            rho_bc = self.rho[:, j:j + 1].to_broadcast([128, CH])
            for b in range(TC // S):
                for tcn in range(S // CH):
                    c0 = b * S + tcn * CH
                    ls = slice(tcn * CH, (tcn + 1) * CH)
                    ut = self.ut[it % 2]
                    yv = self.yv[it % 2]
                    it += 1
                    k.dma(ut[:, :], s5u[32 * j:32 * j + 32, c0:c0 + CH], r=[], w=[ut])
                    pr, pi = cx.bank(), cx.bank()
                    k.mm(pr[:, :CH], self.Bl[0][:, :], ut[:, :], r=[self.Bl[0], ut], w=[pr])
                    k.mm(pi[:, :CH], self.Bl[1][:, :], ut[:, :], r=[self.Bl[1], ut], w=[pi])
                    ta, tb, br, bi = self.ta, self.tb, self.br, self.bi
                    k.tt(ta[:, :], pr[:, :CH], self.tabc[:, ls], ALU.mult, r=[pr, self.tabc], w=[ta])
                    k.tt(tb[:, :], pi[:, :CH], self.tabs[:, ls], ALU.mult, r=[pi, self.tabs], w=[tb], eng="vector")
                    k.tt(br[:, :], ta[:, :], tb[:, :], ALU.add, r=[ta, tb], w=[br], eng="gpsimd")
                    k.tt(ta[:, :], pi[:, :CH], self.tabc[:, ls], ALU.mult, r=[pi, self.tabc], w=[ta])
                    k.tt(tb[:, :], pr[:, :CH], self.tabs[:, ls], ALU.mult, r=[pr, self.tabs], w=[tb])
                    k.tt(bi[:, :], ta[:, :], tb[:, :], ALU.subtract, r=[ta, tb], w=[bi], eng="gpsimd")
                    wr, wi = (self.wr, self.wi) if (tcn % 2 == 0) else (self.wr2, self.wi2)
                    pwr, pwi = (self.wr2, self.wi2) if (tcn % 2 == 0) else (self.wr, self.wi)
                    for (w_, pw_, b_) in ((wr, pwr, br), (wi, pwi, bi)):
                        if tcn > 0:
                            k.stt(b_[:, 0:1], pw_[:, CH - 1:CH], self.rho[:, j:j + 1], b_[:, 0:1], ALU.mult, ALU.add,
                                  r=[pw_, self.rho, b_], w=[b_])
                        k.op("vector", lambda e, w_=w_, b_=b_: e.tensor_tensor_scan(
                            out=w_[:, :], data0=rho_bc, data1=b_[:, :], initial=0.0, op0=ALU.mult, op1=ALU.add),
                             r=[self.rho, b_], w=[w_])
                    xr, xi = self.xr, self.xi
                    k.tt(ta[:, :], wr[:, :], self.tabc[:, ls], ALU.mult, r=[wr, self.tabc], w=[ta])
                    k.tt(tb[:, :], wi[:, :], self.tabs[:, ls], ALU.mult, r=[wi, self.tabs], w=[tb], eng="gpsimd")
                    k.tt(xr[:, :], ta[:, :], tb[:, :], ALU.subtract, r=[ta, tb], w=[xr])
                    k.tt(ta[:, :], wr[:, :], self.tabs[:, ls], ALU.mult, r=[wr, self.tabs], w=[ta])
                    k.tt(tb[:, :], wi[:, :], self.tabc[:, ls], ALU.mult, r=[wi, self.tabc], w=[tb], eng="gpsimd")
                    k.tt(xi[:, :], ta[:, :], tb[:, :], ALU.add, r=[ta, tb], w=[xi])
                    py = cx.bank()
                    k.mm(py[0:32, :CH], self.Cl[0][:, :], xr[:, :], r=[self.Cl[0], xr], w=[py], start=True, stop=False,
                         signal=False)
                    k.mm(py[0:32, :CH], self.Cl[1][:, :], xi[:, :], r=[self.Cl[1], xi], w=[py], start=False, stop=True)
                    k.stt(yv[:, :], ut[:, :], self.dcol[:, 0:1], py[0:32, :CH], ALU.mult, ALU.add,
                          r=[ut, self.dcol, py], w=[yv])
                    y2 = self.y2
                    k.act(y2[:, :], yv[:, :], AF.Square, r=[yv], w=[y2])
                    k.ts(y2[:, :], y2[:, :], 0.044715, 1.0, ALU.mult, ALU.add, r=[y2], w=[y2])
                    k.tt(y2[:, :], y2[:, :], yv[:, :], ALU.mult, r=[y2, yv], w=[y2])
                    k.act(y2[:, :], y2[:, :], AF.Sigmoid, r=[y2], w=[y2], scale=1.5957691216057308)
                    k.tt(yv[:, :], yv[:, :], y2[:, :], ALU.mult, r=[yv, y2], w=[yv])
                    k.dma(s5y[32 * j:32 * j + 32, c0:c0 + CH], yv[:, :], r=[yv], w=[s5y.b((j, b, tcn))])
        if self.dbg is not None:
            for nm in ("tabc", "tabs", "rho", "thr", "th", "cs", "sn", "c1r", "c1i", "dt", "br", "bi", "wr", "wi", "xr", "xi"):
                T_ = getattr(self, nm)
                k.dma(self.dbg[nm][:, :], T_[:, :], r=[T_], w=[self.dbg[nm].b(0)])
            for i in range(2):
                k.dma(self.dbg["Bl%d" % i][:, :], self.Bl[i][:, :], r=[self.Bl[i]], w=[self.dbg["Bl%d" % i].b(0)])
                k.dma(self.dbg["Cl%d" % i][:, :], self.Cl[i][:, :], r=[self.Cl[i]], w=[self.dbg["Cl%d" % i].b(0)])


class OddOut:
    def __init__(self, k, cx):
        self.k, self.cx = k, cx
        sb = k.sb
        self.Wo = [sb([128, D], BF16, name="Wo") for _ in range(KT)]
        self.Wg = [sb([128, 256], BF16, name="Wg") for _ in range(2)]
        self.bg = sb([128, 2], F32, name="bg")
        self.xt = [sb([128, KT, NT], F32, name="o_xt") for _ in range(2)]
        self.mt = [sb([128, KT, NT], BF16, name="o_mt") for _ in range(2)]
        self.yf = [sb([128, 2, NT], F32, name="o_yf") for _ in range(2)]
        self.yb = sb([128, 2, NT], BF16, name="o_yb")
        self.sg = sb([128, NT], F32, name="o_sg")

    def load_weights(self, P, o):
        k = self.k
        for kt in range(KT):
            k.dma(self.Wo[kt][:, :], P["od_w_out"][o][kt * 128:(kt + 1) * 128, :], r=[], w=[self.Wo[kt]], q="gpsimd")
        for kt in range(2):
            k.dma(self.Wg[kt][:, :], P["s5_w_glu"][o][kt * 128:(kt + 1) * 128, :], r=[], w=[self.Wg[kt]], q="gpsimd")
        load_col(k, self.bg, P["s5_b_glu"][o], 256, r=[])

    def run(self, src, dst, mixT, s5y, TC):
        k, cx = self.k, self.cx
        for n in range(TC // NT):
            t0 = n * NT
            xt, mt, yf = self.xt[n % 2], self.mt[n % 2], self.yf[n % 2]
            k.dma(xt[:, :, :], src[:, t0:t0 + NT].rearrange("(kt p) n -> p kt n", p=128), r=src.bs(t0, t0 + NT), w=[xt])
            k.dma(mt[:, 2:8, :], mixT[256:1024, t0:t0 + NT].rearrange("(kt p) n -> p kt n", p=128), r=[], w=[mt])
            k.dma(yf[:, :, :], s5y[:, t0:t0 + NT].rearrange("(kt p) n -> p kt n", p=128), r=[], w=[yf])
            k.copy(self.yb[:, :, :], yf[:, :, :], r=[yf], w=[self.yb])
            for m in range(2):
                pg = cx.bank()
                for kt in range(2):
                    k.mm(pg[:, :NT], self.Wg[kt][:, m * 128:(m + 1) * 128], self.yb[:, kt, :], r=[self.Wg[kt], self.yb],
                         w=[pg], start=(kt == 0), stop=(kt == 1), signal=(kt == 1))
                k.act(self.sg[:, :], pg[:, :NT], AF.Sigmoid, r=[pg, self.bg], w=[self.sg], bias=self.bg[:, m:m + 1])
                k.tt(mt[:, m, :], yf[:, m, :], self.sg[:, :], ALU.mult, r=[yf, self.sg], w=[mt], append=True)
            for m in range(KT):
                po = cx.bank()
                for kt in range(KT):
                    k.mm(po[:, :NT], self.Wo[kt][:, m * 128:(m + 1) * 128], mt[:, kt, :], r=[self.Wo[kt], mt], w=[po],
                         start=(kt == 0), stop=(kt == KT - 1), signal=(kt == KT - 1))
                k.tt(xt[:, m, :], xt[:, m, :], po[:, :NT], ALU.add, r=[xt, po], w=[xt])
            k.dma(dst[:, t0:t0 + NT].rearrange("(kt p) n -> p kt n", p=128), xt[:, :, :], r=[xt], w=dst.bs(t0, t0 + NT))


def odd_mixer_layer(k, cx, P, o, src, dst, S, TC, scr, dbg=None):
    import os
    mask = os.environ.get("ODD_STAGES", "123")
    if "1" in mask:
        with k.scope():
            op_ = OddProj(k, cx, o)
            op_.load_weights(P)
            op_.run(src, S, TC, scr["s5u"], scr["mixT"])
    if "2" in mask:
        with k.scope():
            s5 = S5Stage(k, cx, o, S)
            s5.run(P, TC, scr["s5u"], scr["s5y"], dbg=dbg)
    if "3" in mask:
        with k.scope():
            oo = OddOut(k, cx)
            oo.load_weights(P, o)
            oo.run(src, dst, scr["mixT"], scr["s5y"], TC)


B_FULL, S_FULL, N_CORES = 16, 4096, 8
DEPTH = 4
W_NAMES = ["ffn1_norm", "ffn1_w13", "ffn1_w2", "mix_norm", "ffn2_norm", "ffn2_w13", "ffn2_w2", "ev_w_in", "ev_w_out",
           "rw_mu", "rw_w0", "rw_w2", "rw_a0", "rw_a2", "rw_g2", "rw_k_k", "rw_k_a", "rw_r_k", "rw_ln_w", "rw_ln_b",
           "rw_v0", "rw_v1", "rw_v2", "sb_q_gain", "sb_k_gain", "od_w_in", "od_w_out", "s5_a_re", "s5_a_im", "s5_b_re",
           "s5_b_im", "s5_c_re", "s5_c_im", "s5_d", "s5_log_dt", "s5_w_glu", "s5_b_glu", "hg_lb", "hg_gain"]


def ffn_layer(k, cx, w13, w2, gain, src, dst, TC):
    with k.scope():
        f = FFN(k, cx)
        f.load_weights(w13, w2, gain)
        f.run(src, dst, TC)


def build_program(shapes, S, BC, depth=DEPTH):
    TC = S * BC
    k = K()
    cx = Ctx(k)
    build_consts(k, cx)
    P = {nm: k.nc.dram_tensor(nm, list(shapes[nm]), F32, kind="ExternalInput").ap() for nm in W_NAMES}
    x_in = DR(k, "xT", [D, TC], F32, kind="ExternalInput")
    y_out = DR(k, "yT", [D, TC], F32, kind="ExternalOutput")
    pp = [DR(k, "xa", [D, TC], F32), DR(k, "xb", [D, TC], F32)]
    scr = dict(vfirst=DR(k, "vfirst", [512, TC], F32), sbq=DR(k, "sbq", [512, TC], BF16),
               sbk=DR(k, "sbk", [512, TC], BF16), sbv=DR(k, "sbv", [TC, 512], BF16),
               mixT=DR(k, "mixT", [1024, TC], BF16), s5u=DR(k, "s5u", [256, TC], F32),
               s5y=DR(k, "s5y", [256, TC], F32))
    n_stage = 3 * depth
    bufs = [x_in] + [pp[i % 2] for i in range(n_stage - 1)] + [y_out]
    si = 0
    for layer in range(depth):
        ffn_layer(k, cx, P["ffn1_w13"][layer], P["ffn1_w2"][layer], P["ffn1_norm"][layer], bufs[si], bufs[si + 1], TC)
        si += 1
        if layer % 2 == 0:
            even_mixer_layer(k, cx, P, layer // 2, bufs[si], bufs[si + 1], S, TC, scr)
        else:
            odd_mixer_layer(k, cx, P, layer // 2, bufs[si], bufs[si + 1], S, TC, scr)
        si += 1
        ffn_layer(k, cx, P["ffn2_w13"][layer], P["ffn2_w2"][layer], P["ffn2_norm"][layer], bufs[si], bufs[si + 1], TC)
        si += 1
    k.finish(list(y_out.bufs.values()))
    return k


def kernel(**inputs):
    x = np.asarray(inputs["x"], dtype=np.float32)
    B, S, Dm = x.shape
    BC = B // N_CORES
    shapes = {nm: tuple(np.asarray(inputs[nm]).shape) for nm in W_NAMES}
    k = build_program(shapes, S, BC)
    weights = {nm: np.ascontiguousarray(np.asarray(inputs[nm], dtype=np.float32)) for nm in W_NAMES}
    in_maps = []
    for c in range(N_CORES):
        xT = np.ascontiguousarray(x[c * BC:(c + 1) * BC].reshape(BC * S, Dm).T)
        m = dict(weights)
        m["xT"] = xT
        in_maps.append(m)
    res = run_bass_kernel_spmd(k.nc, in_maps, core_ids=list(range(N_CORES)))
    out = np.empty((B, S, Dm), dtype=np.float32)
    for c in range(N_CORES):
        yT = np.asarray(res.results[c]["yT"], dtype=np.float32)
        out[c * BC:(c + 1) * BC] = yT.T.reshape(BC, S, Dm)
    return out
```
